# Optimizing a Trainium2 kernel written in Bass

```python
import math
import jax, jax.numpy as jnp
from jax import lax
import numpy as np

D_MODEL = 1024
BATCH = 32
SEQ = 256
DEPTH = 4
DEC_BATCH = 4
DEC_SEQ = 1024
PAST_LEN = 256

GRID_W = 64
HEAD_DIM = 64
N_Q_HEADS = 8
N_KV_HEADS = 2
Q_PER_KV = N_Q_HEADS // N_KV_HEADS
ATTN_W = N_Q_HEADS * HEAD_DIM
KV_W = N_KV_HEADS * HEAD_DIM
POOL_WINDOWS = (2, 4, 8, 16)
N_POOL_GROUPS = len(POOL_WINDOWS)
POOL_W = D_MODEL // 2
POOL_GW = POOL_W // N_POOL_GROUPS
N_BRANCHES = 2
IN_COLS = ATTN_W + 2 * KV_W + POOL_W + N_BRANCHES * D_MODEL
FFN_HIDDEN = ((8 * D_MODEL + 3 * 256 - 1) // (3 * 256)) * 256
N_MOD = 6
Q_BLOCK = 128
ROPE_THETA = 10000.0
EPS = 1e-6

kernel_name = 'hybrid_gqa_pool_prefix_dit_step'


def rmsnorm(x, g):
    xf = x.astype(jnp.float32)
    y = xf * lax.rsqrt(jnp.mean(xf * xf, axis=-1, keepdims=True) + EPS)
    return (y * g.astype(jnp.float32)).astype(x.dtype)


def grid_rope(n):
    rows = n // GRID_W
    row = jnp.broadcast_to(jnp.arange(rows, dtype=jnp.int32)[:, None], (rows, GRID_W)).reshape(-1)
    col = jnp.broadcast_to(jnp.arange(GRID_W, dtype=jnp.int32)[None, :], (rows, GRID_W)).reshape(-1)
    n_freq = HEAD_DIM // 4
    inv = ROPE_THETA ** (-jnp.arange(n_freq, dtype=jnp.float32) / n_freq)
    ang = jnp.concatenate([row.astype(jnp.float32)[:, None] * inv[None, :],
                           col.astype(jnp.float32)[:, None] * inv[None, :]], axis=-1)
    return jnp.cos(ang), jnp.sin(ang)


def apply_rope(x, cos, sin):
    xf = x.astype(jnp.float32)
    x1, x2 = jnp.split(xf, 2, axis=-1)
    c = cos[None, :, None, :]
    s = sin[None, :, None, :]
    return jnp.concatenate([x1 * c - x2 * s, x2 * c + x1 * s], axis=-1).astype(x.dtype)


def block_attention(q, k, v):
    b, n = q.shape[0], q.shape[1]
    blk = math.gcd(n, Q_BLOCK)
    nb = n // blk
    qb = q.reshape(b, nb, blk, N_KV_HEADS, Q_PER_KV, HEAD_DIM).transpose(1, 0, 2, 3, 4, 5)
    scale = HEAD_DIM ** -0.5

    def one_block(qi):
        s = jnp.einsum('bqhgd,bkhd->bhgqk', qi, k).astype(jnp.float32) * scale
        p = jax.nn.softmax(s, axis=-1).astype(v.dtype)
        return jnp.einsum('bhgqk,bkhd->bqhgd', p, v)

    o = lax.map(one_block, qb)
    return o.transpose(1, 0, 2, 3, 4, 5).reshape(b, n, ATTN_W)


def multiscale_pool(p, w_pool, pool_scale):
    b, n, _ = p.shape
    pf = p.astype(jnp.float32)
    csum = jnp.concatenate([jnp.zeros((b, 1, POOL_W), jnp.float32), jnp.cumsum(pf, axis=1)], axis=1)
    t = jnp.arange(n, dtype=jnp.int32)
    outs = []
    for g, w in enumerate(POOL_WINDOWS):
        left = w // 2
        right = w - 1 - left
        lo = jnp.clip(t - left, 0, n)
        hi = jnp.clip(t + right + 1, 0, n)
        sl = slice(g * POOL_GW, (g + 1) * POOL_GW)
        cs = csum[:, :, sl]
        mean = (cs[:, hi] - cs[:, lo]) / (hi - lo).astype(jnp.float32)[None, :, None]
        d = mean - pf[:, :, sl]
        outs.append(jnp.einsum('bnc,cd->bnd', d, w_pool[g].astype(jnp.float32)))
    y = jnp.concatenate(outs, axis=-1) * pool_scale.astype(jnp.float32)
    return y.astype(p.dtype)


def trunk_layer(x, mod, rope, ctx_k, ctx_v, w_in, q_norm, k_norm, w_attn_up, w_pool, pool_scale,
                w_pool_up, w_out, n_pre_mix, n_post_mix, n_pre_ffn, n_post_ffn,
                w_ffn_gate, w_ffn_up, w_ffn_down):
    b, n, _ = x.shape
    sh1, sc1, g1, sh2, sc2, g2 = jnp.split(mod[:, None, :], N_MOD, axis=-1)
    h = rmsnorm(x, n_pre_mix) * (1 + sc1) + sh1
    proj = h @ w_in
    q, k, v, p, gates = jnp.split(
        proj, [ATTN_W, ATTN_W + KV_W, ATTN_W + 2 * KV_W, ATTN_W + 2 * KV_W + POOL_W], axis=-1)
    q = rmsnorm(q.reshape(b, n, N_Q_HEADS, HEAD_DIM), q_norm)
    k = rmsnorm(k.reshape(b, n, N_KV_HEADS, HEAD_DIM), k_norm)
    v = v.reshape(b, n, N_KV_HEADS, HEAD_DIM)
    if rope is None:
        keys, vals = k, v
    else:
        cos, sin = rope
        q = apply_rope(q, cos, sin)
        k_lat = apply_rope(k, cos, sin)
        keys = jnp.concatenate([ctx_k, k_lat], axis=1)
        vals = jnp.concatenate([ctx_v, v], axis=1)
    attn = block_attention(q, keys, vals) @ w_attn_up
    pool = multiscale_pool(p, w_pool, pool_scale) @ w_pool_up
    g_attn, g_pool = jnp.split(jax.nn.sigmoid(gates), N_BRANCHES, axis=-1)
    mixed = (g_attn * attn + g_pool * pool) @ w_out
    x = x + g1 * rmsnorm(mixed, n_post_mix)
    h2 = rmsnorm(x, n_pre_ffn) * (1 + sc2) + sh2
    f = (jax.nn.silu(h2 @ w_ffn_gate) * (h2 @ w_ffn_up)) @ w_ffn_down
    x = x + g2 * rmsnorm(f, n_post_ffn)
    return x, k, v


def setup_inputs(seed: int = 0) -> dict:
    key = jax.random.key(seed)
    ks = jax.random.split(key, 32)
    nrm = jax.random.normal
    f32 = jnp.float32

    def gain(k, shape):
        return jnp.ones(shape, f32) + 0.05 * nrm(k, shape, f32)

    return {
        'x_prompt': nrm(ks[0], (BATCH, SEQ, D_MODEL), f32),
        'x_sample': nrm(ks[1], (DEC_BATCH, DEC_SEQ, D_MODEL), f32),
        'cache_k': nrm(ks[2], (DEC_BATCH, DEPTH, PAST_LEN, N_KV_HEADS, HEAD_DIM), f32),
        'cache_v': nrm(ks[3], (DEC_BATCH, DEPTH, PAST_LEN, N_KV_HEADS, HEAD_DIM), f32),
        'c': nrm(ks[4], (DEC_BATCH, D_MODEL), f32),
        'c_ctx': nrm(ks[5], (D_MODEL,), f32),
        'w_ada': nrm(ks[6], (DEPTH, D_MODEL, N_MOD * D_MODEL), f32) * (0.5 * D_MODEL ** -0.5),
        'b_ada': 0.01 * nrm(ks[7], (DEPTH, N_MOD * D_MODEL), f32),
        'w_in': nrm(ks[8], (DEPTH, D_MODEL, IN_COLS), f32) * D_MODEL ** -0.5,
        'q_norm': gain(ks[9], (DEPTH, HEAD_DIM)),
        'k_norm': gain(ks[10], (DEPTH, HEAD_DIM)),
        'w_attn_up': nrm(ks[11], (DEPTH, ATTN_W, D_MODEL), f32) * ATTN_W ** -0.5,
        'w_pool': nrm(ks[12], (DEPTH, N_POOL_GROUPS, POOL_GW, POOL_GW), f32) * POOL_GW ** -0.5,
        'pool_scale': gain(ks[13], (DEPTH, POOL_W)),
        'w_pool_up': nrm(ks[14], (DEPTH, POOL_W, D_MODEL), f32) * POOL_W ** -0.5,
        'w_out': nrm(ks[15], (DEPTH, D_MODEL, D_MODEL), f32) * D_MODEL ** -0.5,
        'n_pre_mix': gain(ks[16], (DEPTH, D_MODEL)),
        'n_post_mix': gain(ks[17], (DEPTH, D_MODEL)),
        'n_pre_ffn': gain(ks[18], (DEPTH, D_MODEL)),
        'n_post_ffn': gain(ks[19], (DEPTH, D_MODEL)),
        'w_ffn_gate': nrm(ks[20], (DEPTH, D_MODEL, FFN_HIDDEN), f32) * D_MODEL ** -0.5,
        'w_ffn_up': nrm(ks[21], (DEPTH, D_MODEL, FFN_HIDDEN), f32) * D_MODEL ** -0.5,
        'w_ffn_down': nrm(ks[22], (DEPTH, FFN_HIDDEN, D_MODEL), f32) * FFN_HIDDEN ** -0.5,
    }


def reference(x_prompt, x_sample, cache_k, cache_v, c, c_ctx, w_ada, b_ada, w_in, q_norm, k_norm,
              w_attn_up, w_pool, pool_scale, w_pool_up, w_out, n_pre_mix, n_post_mix, n_pre_ffn,
              n_post_ffn, w_ffn_gate, w_ffn_up, w_ffn_down):
    rope = grid_rope(x_sample.shape[1])
    silu_ctx = jax.nn.silu(c_ctx)[None, :]
    silu_c = jax.nn.silu(c)
    y = x_prompt
    z = x_sample
    new_ks = []
    new_vs = []
    for l in range(DEPTH):
        lw = (w_in[l], q_norm[l], k_norm[l], w_attn_up[l], w_pool[l], pool_scale[l], w_pool_up[l],
              w_out[l], n_pre_mix[l], n_post_mix[l], n_pre_ffn[l], n_post_ffn[l],
              w_ffn_gate[l], w_ffn_up[l], w_ffn_down[l])
        mod_ctx = silu_ctx @ w_ada[l] + b_ada[l]
        mod_lat = silu_c @ w_ada[l] + b_ada[l]
        y, k_ctx, v_ctx = trunk_layer(y, mod_ctx, None, None, None, *lw)
        new_ks.append(k_ctx)
        new_vs.append(v_ctx)
        z, _, _ = trunk_layer(z, mod_lat, rope, cache_k[:, l], cache_v[:, l], *lw)
    new_k = jnp.stack(new_ks, axis=1)
    new_v = jnp.stack(new_vs, axis=1)
    return (y, z, new_k, new_v)
```

```python
import numpy as np
import ml_dtypes
import concourse.bass as bass
import concourse.mybir as mybir
from concourse.bass_utils import run_bass_kernel_spmd

F32 = mybir.dt.float32
BF16 = mybir.dt.bfloat16
AF = mybir.ActivationFunctionType
ALU = mybir.AluOpType

D = 1024
T = 1536
TS = 1024
NCH = 3
CH = 512
KC = 8
DEPTH = 4
FFN = 2816
NJ = 22
INC = 3328
EPS = 1e-6
NEG = -30000.0
SLOT = 4096
NSLOT = 3
NBANK = 5


class Buf:
    __slots__ = ("name", "w", "r", "aliases")

    def __init__(self, name):
        self.name = name
        self.w = None
        self.r = {}
        self.aliases = []


class Op:
    __slots__ = ("eng", "fn", "deps", "sig", "seq", "epoch", "sigidx", "dkey", "dval", "_waw")

    def __init__(self, eng, fn):
        self.eng = eng
        self.fn = fn
        self.deps = []
        self.sig = False
        self.dkey = None
        self.dval = 0
        self._waw = False


class Prog:
    ENGS = ("pe", "act", "dve", "pool", "sp")

    def __init__(self):
        self.ops = {e: [] for e in self.ENGS}
        self.epoch = 0
        self.dma_count = {}
        self.final_waits = []

    def _dep(self, op, p):
        if p is None or p is op:
            return
        if p.dkey is None and p.eng == op.eng:
            return
        if p.dkey is not None and p.dkey == op.dkey and p.eng == op.eng and op._waw:
            return
        op.deps.append(p)
        p.sig = True

    def op(self, eng, fn, reads=(), writes=(), dma=None, final=False):
        o = Op(eng, fn)
        o.epoch = self.epoch
        o.seq = len(self.ops[eng])
        if dma is not None:
            dma = f"{dma}_e{self.epoch}"
            o.dkey = dma
            self.dma_count[dma] = self.dma_count.get(dma, 0) + 16
            o.dval = self.dma_count[dma]
        for b in reads:
            self._dep(o, b.w)
            for a in b.aliases:
                self._dep(o, a.w)
        for b in writes:
            for bb in [b] + b.aliases:
                o._waw = True
                self._dep(o, bb.w)
                o._waw = False
                for r in bb.r.values():
                    self._dep(o, r)
        for b in reads:
            b.r[eng if dma is None else ("dma", id(o))] = o
        for b in writes:
            b.w = o
            b.r = {}
            for a in b.aliases:
                a.r = {}
        self.ops[eng].append(o)
        if final:
            self.final_waits.append(o)
        return o

    def emit(self, nc, block, stack):
        engsem = {}
        dmasem = {}
        for e in self.ENGS:
            cnt = {}
            for o in self.ops[e]:
                if o.dkey is None and o.sig:
                    cnt[o.epoch] = cnt.get(o.epoch, 0) + 1
                    o.sigidx = cnt[o.epoch]
                    if (e, o.epoch) not in engsem:
                        engsem[(e, o.epoch)] = stack.enter_context(nc.semaphore(f"s_{e}_{o.epoch}"))
        for k in self.dma_count:
            dmasem[k] = stack.enter_context(nc.semaphore(f"d_{k}"))

        def run(e, eng):
            waited = {}
            dwaited = {}
            for o in self.ops[e]:
                for p in o.deps:
                    if p.dkey is not None:
                        if dwaited.get(p.dkey, 0) >= p.dval:
                            continue
                        dwaited[p.dkey] = p.dval
                        eng.wait_ge(dmasem[p.dkey], p.dval)
                    else:
                        if waited.get(p.eng, -1) >= p.seq:
                            continue
                        waited[p.eng] = p.seq
                        eng.wait_ge(engsem[(p.eng, p.epoch)], p.sigidx)
                ins = o.fn(eng)
                if o.dkey is not None:
                    ins.then_inc(dmasem[o.dkey], 16)
                elif o.sig:
                    ins.then_inc(engsem[(e, o.epoch)], 1)
            if e == "sp":
                for k in sorted({o.dkey for o in self.final_waits}):
                    eng.wait_ge(dmasem[k], self.dma_count[k])

        return run, engsem, dmasem


def build_nc(n_layers=DEPTH, debug=False):
    from contextlib import ExitStack

    nc = bass.Bass("TRN2", target_bir_lowering=False)
    dr = {}

    def din(name, shape, dt=F32):
        dr[name] = nc.dram_tensor(name, list(shape), dt, kind="ExternalInput").ap()
        return dr[name]

    def dout(name, shape, dt=F32):
        dr[name] = nc.dram_tensor(name, list(shape), dt, kind="ExternalOutput").ap()
        return dr[name]

    x_in = din("xT", [D, T])
    w_ada = [din(f"w_ada{l}", [D, 6 * D]) for l in range(DEPTH)]
    w_in = [din(f"w_in{l}", [D, INC]) for l in range(DEPTH)]
    w_aup = [din(f"w_attn_up{l}", [512, D]) for l in range(DEPTH)]
    w_pool = [din(f"w_pool{l}", [4, 128, 128]) for l in range(DEPTH)]
    w_pup = [din(f"w_pool_up{l}", [512, D]) for l in range(DEPTH)]
    w_out = [din(f"w_out{l}", [D, D]) for l in range(DEPTH)]
    w_g = [din(f"w_ffn_gate{l}", [D, FFN]) for l in range(DEPTH)]
    w_u = [din(f"w_ffn_up{l}", [D, FFN]) for l in range(DEPTH)]
    w_dn = [din(f"w_down_r{l}", [8, 128, NJ * 128]) for l in range(DEPTH)]
    cvec_d = din("cvec", [128, KC * 2])
    bada_d = din("bada", [128, DEPTH * 48])
    gains_d = din("gains", [128, DEPTH * 4 * KC])
    qkg_d = din("qkg", [128, DEPTH * 2])
    pscale_d = din("pscale", [128, DEPTH * 4])
    cos_d = din("cosT", [128, TS])
    sin_d = din("sinT", [128, TS])
    bias_d = din("biasT", [128, 40])
    ck_d = din("cacheK", [DEPTH, 128, 2 * 256])
    cv_d = din("cacheV", [DEPTH, 128, 2 * 2 * 64])
    mt_d = din("mtab", [6, 128, 3072], BF16)
    cm_d = din("cmats", [128, 3 * 128], BF16)
    y_out = dout("yT", [D, T])
    k_out = dout("kT_out", [DEPTH, 128, T])
    v_out = dout("v_out", [DEPTH, T, 128])

    if debug:
        dbg_q = dout("dbg_q", [128, 4 * T], BF16)
        dbg_OT = dout("dbg_OT", [128, 4 * T], BF16)
        dbg_yT = dout("dbg_yT", [128, 4 * T], BF16)
        dbg_mT = dout("dbg_mT", [128, 8 * T], BF16)
        dbg_x1 = dout("dbg_x1", [128, 8 * T], F32)

    P = Prog()
    stack = ExitStack()

    def sb(name, shape, dt):
        return stack.enter_context(nc.sbuf_tensor(name, list(shape), dt))

    xT = sb("xTs", [128, KC * T], F32)
    hT = sb("hTs", [128, KC * T], BF16)
    U = sb("U", [128, 35072], BF16)
    ring = sb("ring", [128, NSLOT * SLOT], BF16)
    fscr = sb("fscr", [128, 8 * CH], F32)
    bscr = sb("bscr", [128, 8 * CH], BF16)
    cosT = sb("cosTs", [128, TS], F32)
    sinT = sb("sinTs", [128, TS], F32)
    biasT = sb("biasTs", [128, 40], F32)
    cvec = sb("cvecs", [128, KC * 2], F32)
    siluc = sb("siluc", [128, KC * 2], BF16)
    bada = sb("badas", [128, DEPTH * 48], F32)
    gains = sb("gainss", [128, DEPTH * 4 * KC], F32)
    qkg = sb("qkgs", [128, DEPTH * 2], F32)
    pscale = sb("pscales", [128, DEPTH * 4], F32)
    cmats = sb("cmatss", [128, 3 * 128], BF16)
    modT = sb("modT", [128, 2 * 96], F32)
    eff = sb("eff", [128, 2 * 96], F32)
    epsT = sb("epsT", [128, 1], F32)
    wplT = sb("wplT", [128, 512], BF16)
    banks = [stack.enter_context(nc.psum_tensor(f"bank{i}", [128, CH], F32)) for i in range(8)]

    B = {}

    def buf(*key):
        if key not in B:
            B[key] = Buf(str(key))
        return B[key]

    def alias(a, b):
        a.aliases.append(b)
        b.aliases.append(a)

    bankb = [buf("bank", i) for i in range(8)]
    slotb = [buf("slot", i) for i in range(NSLOT)]
    fsb = [buf("fscr", i) for i in range(8)]
    bsb = [buf("bscr", i) for i in range(8)]
    cnt = {"bank": 0, "slot": 0, "f": 0, "b": 0, "acc": 0}

    def nacc():
        i = 5 + cnt["acc"] % 2
        cnt["acc"] += 1
        return banks[i], bankb[i]

    wide = {"n": NBANK}

    def nbank():
        i = cnt["bank"] % wide["n"]
        cnt["bank"] += 1
        return banks[i], bankb[i]

    def nslot():
        i = cnt["slot"] % NSLOT
        cnt["slot"] += 1
        return ring[:, i * SLOT:(i + 1) * SLOT], slotb[i], i

    def nf():
        i = cnt["f"] % 8
        cnt["f"] += 1
        return fscr[:, i * CH:(i + 1) * CH], fsb[i]

    def nb():
        i = cnt["b"] % 8
        cnt["b"] += 1
        return bscr[:, i * CH:(i + 1) * CH], bsb[i]

    O_OT, O_YT, O_MT, O_Q, O_DT, O_KT, O_KTD, O_VA = 0, 6144, 12288, 12288, 18432, 24576, 26112, 29696
    KTDW = 1792

    def xv(kc, t0, n):
        return xT[:, kc * T + t0: kc * T + t0 + n]

    def hv(kc, t0, n):
        return hT[:, kc * T + t0: kc * T + t0 + n]

    def uv(off, j, t0, n, p0=0, p1=128):
        return U[p0:p1, off + j * T + t0: off + j * T + t0 + n]

    def ptok(i):
        return U[:, O_OT + i * 512: O_OT + (i + 1) * 512]

    def ktd(g, p0, p1, k0, n):
        return U[p0:p1, O_KTD + g * KTDW + k0: O_KTD + g * KTDW + k0 + n]

    def vaug(tile_, g, c0, n):
        o = O_VA + (tile_ * 2 + g) * 192 + c0
        return U[:, o:o + n]

    def aTv(j, t0, n):
        return U[:, j * T + t0: j * T + t0 + n]

    fs_mix = U[:, 0:8192].bitcast(F32)
    fs_ffn = hT[:, 0:8192].bitcast(F32)

    def bq(j, c):
        return buf("q", j, c)

    def bh(kc, c):
        return buf("h", kc, c)

    def bx(kc, c):
        return buf("x", kc, c)

    mixer_bufs = []

    def mb(*key):
        b = buf(*key)
        if b not in mixer_bufs:
            mixer_bufs.append(b)
        return b

    aT_all = buf("aT_all")

    def reg_alias_all():
        for b in mixer_bufs:
            if aT_all not in b.aliases:
                alias(b, aT_all)

    ones_m = cmats[:, 0:128]
    bd_m = cmats[:, 128:256]
    rot_m = cmats[:, 256:384]

    def dma_load(q, out_ap, in_ap, wbufs, key, rbufs=(), final=False):
        return P.op(q, lambda e, o=out_ap, i=in_ap: e.dma_start(out=o, in_=i), reads=rbufs, writes=wbufs, dma=key, final=final)

    def mm(out_ap, lhsT, rhs, start, stop, reads, writes):
        return P.op("pe", lambda e, o=out_ap, l=lhsT, r=rhs, s=start, t=stop: e.matmul(o, l, r, start=s, stop=t),
                    reads=reads, writes=writes)

    def act(out_ap, in_ap, func, reads, writes, bias=None, scale=None):
        kw = {}
        if bias is not None:
            kw["bias"] = bias
        if scale is not None:
            kw["scale"] = scale
        return P.op("act", lambda e, o=out_ap, i=in_ap, f=func, k=kw: e.activation(o, i, f, **k),
                    reads=reads, writes=writes)

    def dve(fn, reads, writes):
        return P.op("dve", fn, reads=reads, writes=writes)

    def load_slab(src_ap, shape3):
        sl, sbuf_, si = nslot()
        a, b = shape3
        dst = sl[:, 0:a * b].rearrange("p (a b) -> p a b", b=b)
        dma_load("pool", dst, src_ap, [sbuf_], f"slot{si}")
        return dst, sbuf_

    def rstd_from_bank(bank, bbank, scale):
        t1, t1b = nf()
        act(t1, bank[:], AF.Ln, [bbank], [t1b], bias=epsT[:, 0:1], scale=scale)
        t2, t2b = nf()
        act(t2, t1, AF.Exp, [t1b], [t2b], scale=-0.5)
        return t2, t2b

    def norm_mod(l, which, parity, chunks=(0, 1, 2)):
        for c in chunks:
            v = 0 if c < 2 else 1
            t0 = c * CH
            bank, bb = nbank()
            for kc in range(KC):
                sq, sqb = nb()
                act(sq, xv(kc, t0, CH), AF.Square, [bx(kc, c)], [sqb])
                mm(bank[:], ones_m, sq, kc == 0, kc == KC - 1, [sqb], [bb])
            rs, rsb = rstd_from_bank(bank, bb, 1.0 / D)
            for kc in range(KC):
                A = eff[:, parity * 96 + (0 if which == 0 else 2) * 16 + kc * 2 + v: parity * 96 + (0 if which == 0 else 2) * 16 + kc * 2 + v + 1]
                mrow = (0 if which == 0 else 3) * 8 + kc
                sh = modT[:, parity * 96 + mrow * 2 + v: parity * 96 + mrow * 2 + v + 1]
                tmp, tb = nf()
                dve(lambda e, o=tmp, i0=xv(kc, t0, CH), s=A, i1=rs: e.scalar_tensor_tensor(o, i0, s, i1, ALU.mult, ALU.mult),
                    [bx(kc, c), rsb, buf("eff", parity)], [tb])
                act(hv(kc, t0, CH), tmp, AF.Identity, [tb, buf("mod", parity)], [bh(kc, c)], bias=sh)

    def post_norm_residual(l, which, parity, c, fs, fs_bufs, produce):
        v = 0 if c < 2 else 1
        t0 = c * CH
        ssb, ssbb = nacc()
        pend = None
        for dt in range(KC):
            bank, bb = produce(dt)
            if pend is not None:
                mm(ssb[:], ones_m, pend[0], pend[2] == 0, False, [pend[1]], [ssbb])
            act(fs[:, dt * CH:(dt + 1) * CH], bank[:], AF.Copy, [bb], [fs_bufs[dt]])
            sq, sqb = nb()
            act(sq, bank[:], AF.Square, [bb], [sqb])
            pend = (sq, sqb, dt)
        mm(ssb[:], ones_m, pend[0], False, True, [pend[1]], [ssbb])
        rs, rsb = rstd_from_bank(ssb, ssbb, 1.0 / D)
        for dt in range(KC):
            G = eff[:, parity * 96 + (1 if which == 0 else 3) * 16 + dt * 2 + v: parity * 96 + (1 if which == 0 else 3) * 16 + dt * 2 + v + 1]
            tmp, tb = nf()
            dve(lambda e, o=tmp, i0=fs[:, dt * CH:(dt + 1) * CH], i1=rs: e.tensor_tensor(o, i0, i1, ALU.mult),
                [fs_bufs[dt], rsb], [tb])
            dve(lambda e, o=xv(dt, t0, CH), i0=tmp, s=G: e.scalar_tensor_tensor(o, i0, s, o, ALU.mult, ALU.add),
                [tb, buf("eff", parity), bx(dt, c)], [bx(dt, c)])

    def compute_mod(l, between=None):
        parity = l % 2
        mbank, mbb = banks[7], bankb[7]
        for s in range(12):
            src = w_ada[l].rearrange("(k p) c -> p k c", p=128)[:, :, s * 512:(s + 1) * 512]
            w, wb = load_slab(src, (KC, 512))
            for ct in range(4):
                idx = s * 4 + ct
                for kc in range(KC):
                    mm(mbank[:, idx * 2: idx * 2 + 2], w[:, kc, ct * 128:(ct + 1) * 128], siluc[:, kc * 2: kc * 2 + 2],
                       kc == 0, kc == KC - 1, [wb, buf("siluc")], [mbb])
            if between is not None:
                between(s)
        mo = modT[:, parity * 96:(parity + 1) * 96]
        dve(lambda e, o=mo.rearrange("p (a b) -> p a b", b=2), i0=mbank[:, 0:96].rearrange("p (a b) -> p a b", b=2),
            i1=bada[:, l * 48:(l + 1) * 48].unsqueeze(2).broadcast_to([128, 48, 2]): e.tensor_tensor(o, i0, i1, ALU.add),
            [mbb, buf("consts")], [buf("mod", parity)])

        def gv(n):
            return gains[:, (l * 4 + n) * KC:(l * 4 + n + 1) * KC].unsqueeze(2).broadcast_to([128, KC, 2])

        def mv(m):
            return modT[:, parity * 96 + m * 16: parity * 96 + (m + 1) * 16].rearrange("p (a b) -> p a b", b=2)

        def ev(m):
            return eff[:, parity * 96 + m * 16: parity * 96 + (m + 1) * 16].rearrange("p (a b) -> p a b", b=2)

        rb, wbf = [buf("mod", parity), buf("consts")], [buf("eff", parity)]
        dve(lambda e: e.scalar_tensor_tensor(ev(0), mv(1), 1.0, gv(0), ALU.add, ALU.mult), rb, wbf)
        dve(lambda e: e.tensor_tensor(ev(1), mv(2), gv(1), ALU.mult), rb, wbf)
        dve(lambda e: e.scalar_tensor_tensor(ev(2), mv(4), 1.0, gv(2), ALU.add, ALU.mult), rb, wbf)
        dve(lambda e: e.tensor_tensor(ev(3), mv(5), gv(3), ALU.mult), rb, wbf)

    cb = buf("consts")
    for (dst, src) in ((cvec, cvec_d), (bada, bada_d), (gains, gains_d), (qkg, qkg_d), (pscale, pscale_d),
                       (cosT, cos_d), (sinT, sin_d), (biasT, bias_d), (cmats, cm_d)):
        dma_load("sp", dst[:], src[:, :], [cb], "consts")
    dve(lambda e: e.memset(epsT[:], EPS), [], [buf("eps")])
    act(siluc[:], cvec[:], AF.Silu, [cb, buf("eps")], [buf("siluc")])
    for kc in range(KC):
        dma_load("sp", xT[:, kc * T:(kc + 1) * T], x_in[kc * 128:(kc + 1) * 128, :], [bx(kc, c) for c in range(NCH)], f"xin{kc % 2}")
    compute_mod(0)

    for l in range(n_layers):
        P.epoch = 2 * l + 1
        parity = l % 2
        wl_in = w_in[l].rearrange("(k p) c -> p k c", p=128)

        norm_mod(l, 0, parity)

        ckb = mb("ktd_cache")
        for g in range(2):
            dma_load("pool", ktd(g, 0, 128, 0, 256), ck_d[l][:, g * 256:(g + 1) * 256], [ckb], "cachek")
        vab = [mb("vaug", t_) for t_ in range(14)]
        cvsrc = cv_d[l].rearrange("p (t g d) -> p t g d", t=2, g=2)
        for dup in range(2):
            for t_ in range(2):
                dst = U[:, O_VA + t_ * 384: O_VA + (t_ + 1) * 384].rearrange("p (g c) -> p g c", c=192)[:, :, dup * 128: dup * 128 + 64]
                dma_load("pool", dst, cvsrc[:, t_], [vab[t_]], "cachev")
        va_all = U[:, O_VA:O_VA + 14 * 384].rearrange("p (a c) -> p a c", c=192)[:, :, 64:128]
        dve(lambda e, o=va_all: e.memset(o, 1.0), [], vab)

        gq = qkg[:, l * 2: l * 2 + 1]
        gk = qkg[:, l * 2 + 1: l * 2 + 2]
        wsl = {}

        def get_w(name):
            if name not in wsl:
                if name == "q":
                    wsl[name] = load_slab(wl_in[:, :, 0:512], (KC, 512))
                else:
                    wsl[name] = load_slab(wl_in[:, :, 512:768], (KC, 256))
            return wsl[name]

        def qk_unit(is_k, j, c):
            st = {}
            t0 = c * CH
            gain = gk if is_k else gq
            dest = uv(O_KT, 0, t0, CH) if is_k else uv(O_Q, j, t0, CH)
            destb = mb("kT", c) if is_k else mb("q", j, c)

            def A():
                w, wb = get_w("kv" if is_k else "q")
                col0 = 0 if is_k else j * 128
                bank, bb = nbank()
                for kc in range(KC):
                    mm(bank[:], w[:, kc, col0:col0 + 128], hv(kc, t0, CH), kc == 0, kc == KC - 1, [wb, bh(kc, c)], [bb])
                sq, sqb = nb()
                act(sq, bank[:], AF.Square, [bb], [sqb])
                st.update(bank=bank, bb=bb, sq=sq, sqb=sqb)

            def B_():
                bank, bb = st["bank"], st["bb"]
                b2, b2b = nbank()
                mm(b2[:], bd_m, st["sq"], True, True, [st["sqb"], cb], [b2b])
                rs, rsb = rstd_from_bank(b2, b2b, 1.0 / 64)
                if is_k:
                    kf, kfb = nf()
                    dve(lambda e, o=kf, i0=bank[:], s=gain, i1=rs: e.scalar_tensor_tensor(o, i0, s, i1, ALU.mult, ALU.mult),
                        [bb, rsb, cb], [kfb])
                    dma_load("sp", k_out[l][:, t0:t0 + CH], kf, [buf("kout")], "kout", rbufs=[kfb], final=True)
                    if c == 2:
                        act(dest, kf, AF.Copy, [kfb], [destb])
                    else:
                        qn, qnb = nb()
                        act(qn, kf, AF.Copy, [kfb], [qnb])
                        st.update(qn=qn, qnb=qnb)
                else:
                    if c == 2:
                        dve(lambda e, o=dest, i0=bank[:], s=gain, i1=rs: e.scalar_tensor_tensor(o, i0, s, i1, ALU.mult, ALU.mult),
                            [bb, rsb, cb], [destb])
                    else:
                        qn, qnb = nb()
                        dve(lambda e, o=qn, i0=bank[:], s=gain, i1=rs: e.scalar_tensor_tensor(o, i0, s, i1, ALU.mult, ALU.mult),
                            [bb, rsb, cb], [qnb])
                        st.update(qn=qn, qnb=qnb)

            def C():
                if c < 2:
                    qn, qnb = st["qn"], st["qnb"]
                    b3, b3b = nbank()
                    mm(b3[:], rot_m, qn, True, True, [qnb, cb], [b3b])
                    t1, t1b = nf()
                    P.op("pool", lambda e, o=t1, i0=qn, i1=cosT[:, t0:t0 + CH]: e.tensor_tensor(o, i0, i1, ALU.mult), reads=[qnb, cb], writes=[t1b])
                    t2, t2b = nf()
                    dve(lambda e, o=t2, i0=b3[:], i1=sinT[:, t0:t0 + CH]: e.tensor_tensor(o, i0, i1, ALU.mult), [b3b, cb], [t2b])
                    dve(lambda e, o=dest, i0=t1, i1=t2: e.tensor_tensor(o, i0, i1, ALU.add), [t1b, t2b], [destb])
                if is_k:
                    k0 = 256 + t0 if c < 2 else 1280
                    kd = mb("ktd", c)
                    for g in range(2):
                        for half in range(2):
                            o = ktd(g, half * 64, half * 64 + 64, k0, CH)
                            i = uv(O_KT, 0, t0, CH, g * 64, g * 64 + 64)
                            if (g + half) % 2 == 1:
                                dve(lambda e, o=o, i=i: e.tensor_copy(o, i), [destb], [kd])
                            else:
                                act(o, i, AF.Copy, [destb], [kd])
            return [A, B_, C]

        def v_unit(tb4):
            def A():
                wkv, wkvb = get_w("kv")
                bank, bb = nbank()
                for ti in range(4):
                    i = tb4 * 4 + ti
                    for kc in range(KC):
                        mm(bank[:, ti * 128:(ti + 1) * 128], hv(kc, i * 128, 128), wkv[:, kc, 128:256], kc == 0, kc == KC - 1,
                           [wkvb, bh(kc, i // 4)], [bb])
                vs, vsb = nf()
                dve(lambda e, o=vs, i=bank[:]: e.tensor_copy(o, i), [bb], [vsb])
                dma_load("sp", v_out[l][tb4 * 512:(tb4 + 1) * 512, :].rearrange("(t p) c -> p t c", p=128),
                         vs.rearrange("p (t c) -> p t c", c=128), [buf("vout")], "vout", rbufs=[vsb], final=True)
                for dup in range(2):
                    dst = U[:, O_VA + (2 + tb4 * 4) * 384: O_VA + (2 + tb4 * 4 + 4) * 384].rearrange(
                        "p (t g c) -> p t g c", g=2, c=192)[:, :, :, dup * 128: dup * 128 + 64]
                    act(dst, vs.rearrange("p (t g d) -> p t g d", g=2, d=64), AF.Copy, [vsb], [vab[2 + tb4 * 4 + ti] for ti in range(4)])
            return [A]

        units = [qk_unit(False, j, c) for j in range(4) for c in range(NCH)]
        units += [qk_unit(True, 0, c) for c in range(NCH)]
        units += [v_unit(t_) for t_ in range(3)]
        LAG = (0, 1, 3)
        for i in range(len(units) + LAG[2]):
            for stage in range(3):
                u = i - LAG[stage]
                if 0 <= u < len(units) and stage < len(units[u]):
                    units[u][stage]()
        wp, wpb = load_slab(wl_in[:, :, 768:1280], (KC, 512))
        for i in range(12):
            bank, bb = nbank()
            for kc in range(KC):
                mm(bank[:], hv(kc, i * 128, 128), wp[:, kc, :], kc == 0, kc == KC - 1, [wpb, bh(kc, i // 4)], [bb])
            if i % 2 == 0:
                act(ptok(i), bank[:], AF.Copy, [bb], [mb("ptok", i)])
            else:
                dve(lambda e, o=ptok(i), i_=bank[:]: e.tensor_copy(o, i_), [bb], [mb("ptok", i)])

        wpl = wplT[:, :].rearrange("p (g d) -> p g d", d=128)
        wplb = buf("wpl")
        dma_load("pool", wpl, w_pool[l].rearrange("g c d -> c g d"), [wplb], "wpl")
        mslabs = {}
        for c in range(NCH):
            for s in (2 * c, 2 * c + 1):
                sl, sbuf_, si = nslot()
                dma_load("pool", sl[:, 0:3072], mt_d[s], [sbuf_], f"slot{si}")
                mslabs[s] = (sl[:, 0:3072].rearrange("p (t g r o) -> p t g r o", t=2, g=4, r=3), sbuf_)
            for g in range(4):
                bank, bb = nbank()
                for ti in range(4):
                    i = c * 4 + ti
                    rr = [r for r in (-1, 0, 1) if not ((r == -1 and i in (0, 8, 10)) or (r == 1 and i in (7, 9, 11)))]
                    ms, msb = mslabs[i // 2]
                    for n, r in enumerate(rr):
                        mm(bank[:, ti * 128:(ti + 1) * 128], ptok(i + r)[:, g * 128:(g + 1) * 128], ms[:, i % 2, g, r + 1, :],
                           n == 0, n == len(rr) - 1, [msb, mb("ptok", i + r)], [bb])
                act(uv(O_DT, g, c * CH, CH), bank[:], AF.Copy, [bb], [mb("dT", g, c)])
                b2, b2b = nbank()
                mm(b2[:], wpl[:, g, :], uv(O_DT, g, c * CH, CH), True, True, [wplb, mb("dT", g, c)], [b2b])
                dve(lambda e, o=uv(O_YT, g, c * CH, CH), i=b2[:], s=pscale[:, l * 4 + g: l * 4 + g + 1]: e.tensor_scalar(o, i, s, None, ALU.mult),
                    [b2b, cb], [mb("yT", g, c)])

        LOOK = 3
        tasks = []

        def mk_norm(h, c, ob, obb):
            j, hh = h // 2, h % 2
            p0, p1 = hh * 64, hh * 64 + 64
            d0, d1 = (64, 128) if hh == 0 else (0, 64)

            def fn():
                l1, l1b = nf()
                dve(lambda e, o=l1[p0:p1, :], i=ob[d0:d1, :]: e.tensor_copy(o, i), [obb], [l1b])
                act(l1[p0:p1, :], l1[p0:p1, :], AF.Ln, [l1b], [l1b])
                l2, l2b = nf()
                act(l2[p0:p1, :], l1[p0:p1, :], AF.Exp, [l1b], [l2b], scale=-1.0)
                dve(lambda e, o=uv(O_OT, j, c * CH, CH, p0, p1), i0=ob[p0:p1, :], i1=l2[p0:p1, :]: e.tensor_tensor(o, i0, i1, ALU.mult),
                    [obb, l2b], [mb("OT", j, c)])
            return fn

        for h in range(8):
            g, j, hh = h // 4, h // 2, h % 2
            p0, p1 = hh * 64, hh * 64 + 64
            vc0 = 0 if hh == 0 else 64
            for c in range(NCH):
                st = {}

                nrm = None
                if c < 2:
                    for kt in range(10):
                        def front(kt=kt, c=c, g=g, j=j, p0=p0, p1=p1, st=st):
                            if kt == 0:
                                st["ob"], st["obb"] = nacc()
                            sbk, sbkb = nbank()
                            kb_ = ckb if kt < 2 else mb("ktd", (kt - 2) // 4)
                            mm(sbk[:], ktd(g, p0, p1, kt * 128, 128), uv(O_Q, j, c * CH, CH, p0, p1), True, True,
                               [kb_, mb("q", j, c)], [sbkb])
                            pT, pTb = nb()
                            for half in range(2):
                                bi = (kt * 2 + c) * 2 + half
                                act(pT[:, half * 256:(half + 1) * 256], sbk[:, half * 256:(half + 1) * 256], AF.Exp, [sbkb, cb], [pTb],
                                    bias=biasT[:, bi:bi + 1], scale=0.125)
                            st[kt] = (pT, pTb)

                        def back(kt=kt, c=c, h=h, g=g, vc0=vc0, st=st):
                            pT, pTb = st[kt]
                            mm(st["ob"][:], vaug(kt, g, vc0, 128), pT, kt == 0, kt == 9, [vab[kt], pTb], [st["obb"]])
                            if kt == 9:
                                mk_norm(h, c, st["ob"], st["obb"])()
                        tasks.append((front, back))
                else:
                    for s_ in range(2):
                        def front(s_=s_, g=g, j=j, p0=p0, p1=p1, st=st):
                            if s_ == 0:
                                st["ob"], st["obb"] = nacc()
                            sbk, sbkb = nbank()
                            pT, pTb = nb()
                            for kt in range(2):
                                mm(sbk[:, kt * 256:(kt + 1) * 256], ktd(g, p0, p1, 1280 + s_ * 256 + kt * 128, 128),
                                   uv(O_Q, j, 1024 + s_ * 256, 256, p0, p1), True, True, [mb("ktd", 2), mb("q", j, 2)], [sbkb])
                            act(pT, sbk[:], AF.Exp, [sbkb], [pTb], scale=0.125)
                            st[s_] = (pT, pTb)

                        def back(s_=s_, h=h, g=g, vc0=vc0, st=st):
                            pT, pTb = st[s_]
                            for kt in range(2):
                                mm(st["ob"][:, s_ * 256:(s_ + 1) * 256], vaug(10 + s_ * 2 + kt, g, vc0, 128), pT[:, kt * 256:(kt + 1) * 256],
                                   kt == 0, kt == 1, [vab[10 + s_ * 2 + kt], pTb], [st["obb"]])
                            if s_ == 1:
                                mk_norm(h, 2, st["ob"], st["obb"])()
                        tasks.append((front, back))
        for i in range(len(tasks) + LOOK):
            if i < len(tasks):
                tasks[i][0]()
            if i >= LOOK:
                tasks[i - LOOK][1]()

        if debug and l == 0:
            P.op("sp", lambda e: e.dma_start(out=dbg_q[:, :], in_=U[:, O_Q:O_Q + 4 * T]), reads=[mb("q", j, c) for j in range(4) for c in range(NCH)], writes=[buf("dbgo")], dma="dbg", final=True)
            P.op("sp", lambda e: e.dma_start(out=dbg_OT[:, :], in_=U[:, O_OT:O_OT + 4 * T]), reads=[mb("OT", j, c) for j in range(4) for c in range(NCH)], writes=[buf("dbgo")], dma="dbg", final=True)
            P.op("sp", lambda e: e.dma_start(out=dbg_yT[:, :], in_=U[:, O_YT:O_YT + 4 * T]), reads=[mb("yT", j, c) for j in range(4) for c in range(NCH)], writes=[buf("dbgo")], dma="dbg", final=True)
        wide["n"] = 7
        waul = w_aup[l].rearrange("(k p) c -> p k c", p=128)
        wpul = w_pup[l].rearrange("(k p) c -> p k c", p=128)
        for dt in range(KC):
            sl, sbuf_, si = nslot()
            wgd = sl[:, 0:2048].rearrange("p (k a c) -> p k a c", a=2, c=128)
            wga = wgd[:, :, 0, :]
            wgp = wgd[:, :, 1, :]
            wau = sl[:, 2048:2560].rearrange("p (k c) -> p k c", c=128)
            wpu = sl[:, 2560:3072].rearrange("p (k c) -> p k c", c=128)
            waub = wpub = sbuf_
            dma_load("pool", wga, wl_in[:, :, 1280 + dt * 128: 1280 + (dt + 1) * 128], [sbuf_], f"slot{si}")
            dma_load("pool", wgp, wl_in[:, :, 2304 + dt * 128: 2304 + (dt + 1) * 128], [sbuf_], f"slot{si}")
            dma_load("pool", wau, waul[:, :, dt * 128:(dt + 1) * 128], [sbuf_], f"slot{si}")
            dma_load("pool", wpu, wpul[:, :, dt * 128:(dt + 1) * 128], [sbuf_], f"slot{si}")
            for c in range(NCH):
                t0 = c * CH
                ba, bab = nbank()
                for kc in range(4):
                    mm(ba[:], wau[:, kc, :], uv(O_OT, kc, t0, CH), kc == 0, kc == 3, [waub, mb("OT", kc, c)], [bab])
                bp, bpb = nbank()
                for kc in range(4):
                    mm(bp[:], wpu[:, kc, :], uv(O_YT, kc, t0, CH), kc == 0, kc == 3, [wpub, mb("yT", kc, c)], [bpb])
                bga, bgab = nbank()
                for kc in range(KC):
                    mm(bga[:], wga[:, kc, :], hv(kc, t0, CH), kc == 0, kc == KC - 1, [sbuf_, bh(kc, c)], [bgab])
                bgp, bgpb = nbank()
                for kc in range(KC):
                    mm(bgp[:], wgp[:, kc, :], hv(kc, t0, CH), kc == 0, kc == KC - 1, [sbuf_, bh(kc, c)], [bgpb])
                sa, sab = nf()
                act(sa, bga[:], AF.Sigmoid, [bgab], [sab])
                sp_, spb = nf()
                act(sp_, bgp[:], AF.Sigmoid, [bgpb], [spb])
                m1, m1b = nf()
                dve(lambda e, o=m1, i0=ba[:], i1=sa: e.tensor_tensor(o, i0, i1, ALU.mult), [bab, sab], [m1b])
                m2, m2b = nf()
                dve(lambda e, o=m2, i0=bp[:], i1=sp_: e.tensor_tensor(o, i0, i1, ALU.mult), [bpb, spb], [m2b])
                dve(lambda e, o=uv(O_MT, dt, t0, CH), i0=m1, i1=m2: e.tensor_tensor(o, i0, i1, ALU.add), [m1b, m2b], [mb("mT", dt, c)])
        fsm_bufs = [mb("fsmix", dt) for dt in range(KC)]
        if l == 0:
            for dt in range(KC):
                for c in range(NCH):
                    for j in range(4):
                        alias(mb("mT", dt, c), mb("q", j, c)) if dt < 4 and j == dt else None
                        alias(mb("mT", dt, c), mb("dT", j, c)) if dt >= 4 and j == dt - 4 else None
                for j in range(4):
                    for c in range(NCH):
                        alias(fsm_bufs[dt], mb("OT", j, c))
                        alias(fsm_bufs[dt], mb("yT", j, c))
                for i in range(12):
                    alias(fsm_bufs[dt], mb("ptok", i))
            for i in range(12):
                for j in range(4):
                    for c in range(NCH):
                        alias(mb("ptok", i), mb("OT", j, c))
            reg_alias_all()

        if debug and l == 0:
            P.op("sp", lambda e: e.dma_start(out=dbg_mT[:, :], in_=U[:, O_MT:O_MT + 8 * T]), reads=[mb("mT", j, c) for j in range(8) for c in range(NCH)], writes=[buf("dbgo")], dma="dbg", final=True)
        wide["n"] = NBANK
        wo0, wo0b = load_slab(w_out[l].rearrange("(k p) c -> p k c", p=128)[:, :, 0:512], (KC, 512))
        wo1, wo1b = load_slab(w_out[l].rearrange("(k p) c -> p k c", p=128)[:, :, 512:1024], (KC, 512))
        for c in range(NCH):
            def prod(dt, c=c):
                bank, bb = nbank()
                w, wb = (wo0, wo0b) if dt < 4 else (wo1, wo1b)
                for kc in range(KC):
                    mm(bank[:], w[:, kc, (dt % 4) * 128:(dt % 4 + 1) * 128], uv(O_MT, kc, c * CH, CH), kc == 0, kc == KC - 1,
                       [wb, mb("mT", kc, c)], [bb])
                return bank, bb
            post_norm_residual(l, 0, parity, c, fs_mix, fsm_bufs, prod)
            if c >= 1:
                norm_mod(l, 1, parity, chunks=(c - 1,))

        if debug and l == 0:
            P.op("sp", lambda e: e.dma_start(out=dbg_x1[:, :], in_=xT[:, :]), reads=[bx(kc, c) for kc in range(KC) for c in range(NCH)], writes=[buf("dbgo")], dma="dbg", final=True)
        P.epoch = 2 * l + 2
        norm_mod(l, 1, parity, chunks=(2,))
        nxt_mod = (l + 1 < n_layers)
        ada_state = {"s": 0}

        def ada_slab(lnext):
            s = ada_state["s"]
            if s >= 12:
                return
            ada_state["s"] += 1
            src = w_ada[lnext].rearrange("(k p) c -> p k c", p=128)[:, :, s * 512:(s + 1) * 512]
            w, wb = load_slab(src, (KC, 512))
            for ct in range(4):
                idx = s * 4 + ct
                for kc in range(KC):
                    mm(banks[7][:, idx * 2: idx * 2 + 2], w[:, kc, ct * 128:(ct + 1) * 128], siluc[:, kc * 2: kc * 2 + 2],
                       kc == 0, kc == KC - 1, [wb, buf("siluc")], [bankb[7]])

        wgl = w_g[l].rearrange("(k p) c -> p k c", p=128)
        wul = w_u[l].rearrange("(k p) c -> p k c", p=128)
        aTb = [[buf("aT", j, c) for c in range(NCH)] for j in range(NJ)]
        if l == 0:
            for j in range(NJ):
                for c in range(NCH):
                    alias(aTb[j][c], aT_all)
        wide["n"] = 7
        for jb in range(NJ // 2):
            sl, sbuf_, si = nslot()
            wgu = sl[:, 0:4096].rearrange("p (k a c) -> p k a c", a=2, c=256)
            dma_load("pool", wgu[:, :, 0, :], wgl[:, :, jb * 256:(jb + 1) * 256], [sbuf_], f"slot{si}")
            dma_load("pool", wgu[:, :, 1, :], wul[:, :, jb * 256:(jb + 1) * 256], [sbuf_], f"slot{si}")
            for jj in range(2):
                j = jb * 2 + jj
                for c in range(NCH):
                    t0 = c * CH
                    bg, bgb = nbank()
                    for kc in range(KC):
                        mm(bg[:], wgu[:, kc, 0, jj * 128:(jj + 1) * 128], hv(kc, t0, CH), kc == 0, kc == KC - 1, [sbuf_, bh(kc, c)], [bgb])
                    bu, bub = nbank()
                    for kc in range(KC):
                        mm(bu[:], wgu[:, kc, 1, jj * 128:(jj + 1) * 128], hv(kc, t0, CH), kc == 0, kc == KC - 1, [sbuf_, bh(kc, c)], [bub])
                    sg, sgb = nf()
                    act(sg, bg[:], AF.Silu, [bgb], [sgb])
                    dve(lambda e, o=aTv(j, t0, CH), i0=sg, i1=bu[:]: e.tensor_tensor(o, i0, i1, ALU.mult), [sgb, bub], [aTb[j][c], aT_all])
            if nxt_mod and jb % 2 == 1:
                ada_slab(l + 1)
        wide["n"] = NBANK
        fsf_bufs = [buf("fsffn", dt) for dt in range(KC)]
        if l == 0:
            for dt in range(KC):
                for kc in range(KC):
                    for c in range(NCH):
                        alias(fsf_bufs[dt], bh(kc, c))
        for c in range(NCH):
            def prodf(dt, c=c):
                w, wb = load_slab(w_dn[l][dt].rearrange("p (j c) -> p j c", c=128), (NJ, 128))
                bank, bb = nbank()
                for j in range(NJ):
                    mm(bank[:], w[:, j, :], aTv(j, c * CH, CH), j == 0, j == NJ - 1, [wb, aTb[j][c], aT_all], [bb])
                if nxt_mod and dt % 2 == 1:
                    ada_slab(l + 1)
                return bank, bb
            post_norm_residual(l, 1, parity, c, fs_ffn, fsf_bufs, prodf)
        if nxt_mod:
            while ada_state["s"] < 12:
                ada_slab(l + 1)
            parity_n = (l + 1) % 2
            mo = modT[:, parity_n * 96:(parity_n + 1) * 96]
            ln = l + 1
            dve(lambda e, o=mo.rearrange("p (a b) -> p a b", b=2), i0=banks[7][:, 0:96].rearrange("p (a b) -> p a b", b=2),
                i1=bada[:, ln * 48:(ln + 1) * 48].unsqueeze(2).broadcast_to([128, 48, 2]): e.tensor_tensor(o, i0, i1, ALU.add),
                [bankb[7], cb], [buf("mod", parity_n)])

            def gv(n, ln=ln):
                return gains[:, (ln * 4 + n) * KC:(ln * 4 + n + 1) * KC].unsqueeze(2).broadcast_to([128, KC, 2])

            def mv(m, pn=parity_n):
                return modT[:, pn * 96 + m * 16: pn * 96 + (m + 1) * 16].rearrange("p (a b) -> p a b", b=2)

            def ev(m, pn=parity_n):
                return eff[:, pn * 96 + m * 16: pn * 96 + (m + 1) * 16].rearrange("p (a b) -> p a b", b=2)

            rb, wbf = [buf("mod", parity_n), cb], [buf("eff", parity_n)]
            dve(lambda e, ev=ev, mv=mv, gv=gv: e.scalar_tensor_tensor(ev(0), mv(1), 1.0, gv(0), ALU.add, ALU.mult), rb, wbf)
            dve(lambda e, ev=ev, mv=mv, gv=gv: e.tensor_tensor(ev(1), mv(2), gv(1), ALU.mult), rb, wbf)
            dve(lambda e, ev=ev, mv=mv, gv=gv: e.scalar_tensor_tensor(ev(2), mv(4), 1.0, gv(2), ALU.add, ALU.mult), rb, wbf)
            dve(lambda e, ev=ev, mv=mv, gv=gv: e.tensor_tensor(ev(3), mv(5), gv(3), ALU.mult), rb, wbf)

    for kc in range(KC):
        P.op("sp", lambda e, kc=kc: e.dma_start(out=y_out[kc * 128:(kc + 1) * 128, :], in_=xT[:, kc * T:(kc + 1) * T]),
             reads=[bx(kc, c) for c in range(NCH)], writes=[buf("yout")], dma="yout", final=True)

    block = stack.enter_context(nc.Block())
    run, _, _ = P.emit(nc, block, stack)

    @block.tensor
    def _(e):
        run("pe", e)

    @block.scalar
    def _(e):
        run("act", e)

    @block.vector
    def _(e):
        run("dve", e)

    @block.gpsimd
    def _(e):
        run("pool", e)

    @block.sync
    def _(e):
        run("sp", e)

    stack.close()
    return nc


POOL_WINDOWS = (2, 4, 8, 16)


def _pool_band(n_seq_tiles_list):
    ntiles = sum(n_seq_tiles_list)
    M = np.zeros((ntiles, 4, 3, 128, 128), np.float32)
    tile0 = 0
    for nt in n_seq_tiles_list:
        n = nt * 128
        for g, w in enumerate(POOL_WINDOWS):
            left = w // 2
            right = w - 1 - left
            for t in range(n):
                lo = max(t - left, 0)
                hi = min(t + right + 1, n)
                val = 1.0 / (hi - lo)
                ti, to = divmod(t, 128)
                for tp in range(lo, hi):
                    tpi, tpo = divmod(tp, 128)
                    M[tile0 + ti, g, tpi - ti + 1, tpo, to] += val
                M[tile0 + ti, g, 1, to, to] -= 1.0
        tile0 += nt
    return M


def _prep(inputs):
    f32 = np.float32
    x_prompt = np.asarray(inputs["x_prompt"], f32)
    x_sample = np.asarray(inputs["x_sample"], f32)
    cache_k = np.asarray(inputs["cache_k"], f32)
    cache_v = np.asarray(inputs["cache_v"], f32)
    c = np.asarray(inputs["c"], f32)
    c_ctx = np.asarray(inputs["c_ctx"], f32)

    shared = {}
    for k in ("w_ada", "w_in", "w_attn_up", "w_pool", "w_pool_up", "w_out", "w_ffn_gate", "w_ffn_up"):
        a = np.asarray(inputs[k], f32)
        for l in range(DEPTH):
            shared[f"{k}{l}"] = np.ascontiguousarray(a[l])
    wd = np.asarray(inputs["w_ffn_down"], f32)
    wdr = wd.reshape(DEPTH, NJ, 128, 8, 128).transpose(0, 3, 2, 1, 4).reshape(DEPTH, 8, 128, NJ * 128)
    for l in range(DEPTH):
        shared[f"w_down_r{l}"] = np.ascontiguousarray(wdr[l])
    b_ada = np.asarray(inputs["b_ada"], f32)
    shared["bada"] = np.ascontiguousarray(b_ada.reshape(DEPTH, 48, 128).transpose(2, 0, 1).reshape(128, DEPTH * 48))
    gs = np.stack([np.asarray(inputs[k], f32) for k in ("n_pre_mix", "n_post_mix", "n_pre_ffn", "n_post_ffn")], 1)
    shared["gains"] = np.ascontiguousarray(gs.reshape(DEPTH, 4, KC, 128).transpose(3, 0, 1, 2).reshape(128, DEPTH * 4 * KC))
    qn = np.asarray(inputs["q_norm"], f32)
    kn = np.asarray(inputs["k_norm"], f32)
    qk = np.stack([qn, kn], 1)
    shared["qkg"] = np.ascontiguousarray(np.tile(qk.transpose(2, 0, 1), (2, 1, 1)).reshape(128, DEPTH * 2))
    ps = np.asarray(inputs["pool_scale"], f32)
    shared["pscale"] = np.ascontiguousarray(ps.reshape(DEPTH, 4, 128).transpose(2, 0, 1).reshape(128, DEPTH * 4))
    ones = np.ones((128, 128), f32)
    bd = np.zeros((128, 128), f32)
    bd[:64, :64] = 1
    bd[64:, 64:] = 1
    rot = np.zeros((128, 128), f32)
    for hb in (0, 64):
        for d in range(32):
            rot[hb + d + 32, hb + d] = -1.0
            rot[hb + d, hb + d + 32] = 1.0
    shared["cmats"] = np.concatenate([ones, bd, rot], 1).astype(ml_dtypes.bfloat16)

    n = TS
    rows = n // 64
    row = np.repeat(np.arange(rows, dtype=f32), 64)
    col = np.tile(np.arange(64, dtype=f32), rows)
    inv = (f32(10000.0) ** (-np.arange(16, dtype=f32) / f32(16))).astype(f32)
    ang = np.concatenate([row[:, None] * inv[None, :], col[:, None] * inv[None, :]], -1).astype(f32)
    cos_s = np.cos(ang).astype(f32)
    sin_s = np.sin(ang).astype(f32)
    idx = np.arange(128) % 32
    cos_sample = np.ascontiguousarray(cos_s[:, idx].T)
    sin_sample = np.ascontiguousarray(sin_s[:, idx].T)
    cos_prompt = np.ones((128, TS), f32)
    sin_prompt = np.zeros((128, TS), f32)

    bias_sample = np.zeros((128, 40), f32)
    bias_prompt = np.full((128, 40), NEG, f32)
    for kt in range(2, 10):
        for cc in range(2):
            for half in range(2):
                if (kt - 2) // 2 == cc * 2 + half:
                    bias_prompt[:, (kt * 2 + cc) * 2 + half] = 0.0

    M_sample = np.concatenate([_pool_band([8]), _pool_band([2, 2])], 0)
    M_prompt = _pool_band([2] * 6)

    def mt_layout(M):
        return np.ascontiguousarray(M.reshape(6, 2, 4, 3, 128, 128).transpose(0, 4, 1, 2, 3, 5).reshape(6, 128, 3072)).astype(ml_dtypes.bfloat16)

    mt_sample = mt_layout(M_sample)
    mt_prompt = mt_layout(M_prompt)

    in_maps = []
    assign = []
    for core in range(8):
        m = dict(shared)
        if core < 4:
            pS = None
            pP = [2 * core, 2 * core + 1]
            xs = np.concatenate([x_sample[core]] + [x_prompt[b] for b in pP], 0)
            cS = c[core]
            m["cosT"], m["sinT"], m["biasT"], m["mtab"] = cos_sample, sin_sample, bias_sample, mt_sample
            ck = cache_k[core]
            ckt = ck.transpose(0, 2, 3, 1)
            ckd = np.concatenate([ckt, ckt], 2)
            m["cacheK"] = np.ascontiguousarray(ckd.transpose(0, 2, 1, 3).reshape(DEPTH, 128, 512))
            cv = cache_v[core]
            m["cacheV"] = np.ascontiguousarray(cv.reshape(DEPTH, 2, 128, 2, 64).transpose(0, 2, 1, 3, 4).reshape(DEPTH, 128, 256))
        else:
            base = 8 + 6 * (core - 4)
            pS = [base + i for i in range(4)]
            pP = [base + 4, base + 5]
            xs = np.concatenate([x_prompt[b] for b in pS + pP], 0)
            cS = c_ctx
            m["cosT"], m["sinT"], m["biasT"], m["mtab"] = cos_prompt, sin_prompt, bias_prompt, mt_prompt
            m["cacheK"] = np.zeros((DEPTH, 128, 512), f32)
            m["cacheV"] = np.zeros((DEPTH, 128, 256), f32)
        m["xT"] = np.ascontiguousarray(xs.T)
        cv2 = np.stack([cS, c_ctx], 1)
        m["cvec"] = np.ascontiguousarray(cv2.reshape(KC, 128, 2).transpose(1, 0, 2).reshape(128, KC * 2))
        in_maps.append(m)
        assign.append((pS, pP))
    return in_maps, assign


_NC_CACHE = {}


def kernel(**inputs):
    in_maps, assign = _prep(inputs)
    if "nc" not in _NC_CACHE:
        _NC_CACHE["nc"] = build_nc()
    nc = _NC_CACHE["nc"]
    res = run_bass_kernel_spmd(nc, in_maps, core_ids=list(range(8)))
    y_prompt = np.zeros((32, 256, D), np.float32)
    y_sample = np.zeros((4, 1024, D), np.float32)
    new_k = np.zeros((32, DEPTH, 256, 2, 64), np.float32)
    new_v = np.zeros((32, DEPTH, 256, 2, 64), np.float32)
    for core in range(8):
        r = res.results[core]
        y = np.asarray(r["yT"]).T
        kt = np.asarray(r["kT_out"])
        vt = np.asarray(r["v_out"])
        pS, pP = assign[core]
        segs = []
        if pS is None:
            y_sample[core] = y[:TS]
        else:
            segs += [(b, i * 256) for i, b in enumerate(pS)]
        segs += [(b, TS + i * 256) for i, b in enumerate(pP)]
        for b, t0 in segs:
            y_prompt[b] = y[t0:t0 + 256]
            new_k[b] = kt[:, :, t0:t0 + 256].transpose(0, 2, 1).reshape(DEPTH, 256, 2, 64)
            new_v[b] = vt[:, t0:t0 + 256, :].reshape(DEPTH, 256, 2, 64)
    return (y_prompt, y_sample, new_k, new_v)
```

```python
import numpy as np
import ml_dtypes
import concourse.bass as bass
import concourse.mybir as mybir
from concourse.bass_utils import run_bass_kernel_spmd

F32 = mybir.dt.float32
BF16 = mybir.dt.bfloat16
AF = mybir.ActivationFunctionType
ALU = mybir.AluOpType

D = 1024
T = 1536
TS = 1024
NCH = 3
CH = 512
KC = 8
DEPTH = 4
FFN = 2816
NJ = 22
INC = 3328
EPS = 1e-6
NEG = -30000.0
SLOT = 4096
NSLOT = 3
NBANK = 5


class Buf:
    __slots__ = ("name", "w", "r", "aliases")

    def __init__(self, name):
        self.name = name
        self.w = None
        self.r = {}
        self.aliases = []


class Op:
    __slots__ = ("eng", "fn", "deps", "sig", "seq", "epoch", "sigidx", "dkey", "dval", "_waw")

    def __init__(self, eng, fn):
        self.eng = eng
        self.fn = fn
        self.deps = []
        self.sig = False
        self.dkey = None
        self.dval = 0
        self._waw = False


class Prog:
    ENGS = ("pe", "act", "dve", "pool", "sp")

    def __init__(self):
        self.ops = {e: [] for e in self.ENGS}
        self.epoch = 0
        self.dma_count = {}
        self.final_waits = []

    def _dep(self, op, p):
        if p is None or p is op:
            return
        if p.dkey is None and p.eng == op.eng:
            return
        if p.dkey is not None and p.dkey == op.dkey and p.eng == op.eng and op._waw:
            return
        op.deps.append(p)
        p.sig = True

    def op(self, eng, fn, reads=(), writes=(), dma=None, final=False):
        o = Op(eng, fn)
        o.epoch = self.epoch
        o.seq = len(self.ops[eng])
        if dma is not None:
            dma = f"{dma}_e{self.epoch}"
            o.dkey = dma
            self.dma_count[dma] = self.dma_count.get(dma, 0) + 16
            o.dval = self.dma_count[dma]
        for b in reads:
            self._dep(o, b.w)
            for a in b.aliases:
                self._dep(o, a.w)
        for b in writes:
            for bb in [b] + b.aliases:
                o._waw = True
                self._dep(o, bb.w)
                o._waw = False
                for r in bb.r.values():
                    self._dep(o, r)
        for b in reads:
            b.r[eng if dma is None else ("dma", id(o))] = o
        for b in writes:
            b.w = o
            b.r = {}
            for a in b.aliases:
                a.r = {}
        self.ops[eng].append(o)
        if final:
            self.final_waits.append(o)
        return o

    def emit(self, nc, block, stack):
        engsem = {}
        dmasem = {}
        for e in self.ENGS:
            cnt = {}
            for o in self.ops[e]:
                if o.dkey is None and o.sig:
                    cnt[o.epoch] = cnt.get(o.epoch, 0) + 1
                    o.sigidx = cnt[o.epoch]
                    if (e, o.epoch) not in engsem:
                        engsem[(e, o.epoch)] = stack.enter_context(nc.semaphore(f"s_{e}_{o.epoch}"))
        for k in self.dma_count:
            dmasem[k] = stack.enter_context(nc.semaphore(f"d_{k}"))

        def run(e, eng):
            waited = {}
            dwaited = {}
            for o in self.ops[e]:
                for p in o.deps:
                    if p.dkey is not None:
                        if dwaited.get(p.dkey, 0) >= p.dval:
                            continue
                        dwaited[p.dkey] = p.dval
                        eng.wait_ge(dmasem[p.dkey], p.dval)
                    else:
                        if waited.get(p.eng, -1) >= p.seq:
                            continue
                        waited[p.eng] = p.seq
                        eng.wait_ge(engsem[(p.eng, p.epoch)], p.sigidx)
                ins = o.fn(eng)
                if o.dkey is not None:
                    ins.then_inc(dmasem[o.dkey], 16)
                elif o.sig:
                    ins.then_inc(engsem[(e, o.epoch)], 1)
            if e == "sp":
                for k in sorted({o.dkey for o in self.final_waits}):
                    eng.wait_ge(dmasem[k], self.dma_count[k])

        return run, engsem, dmasem


def build_nc(n_layers=DEPTH, debug=False):
    from contextlib import ExitStack

    nc = bass.Bass("TRN2", target_bir_lowering=False)
    dr = {}

    def din(name, shape, dt=F32):
        dr[name] = nc.dram_tensor(name, list(shape), dt, kind="ExternalInput").ap()
        return dr[name]

    def dout(name, shape, dt=F32):
        dr[name] = nc.dram_tensor(name, list(shape), dt, kind="ExternalOutput").ap()
        return dr[name]

    x_in = din("xT", [D, T])
    w_ada = [din(f"w_ada{l}", [D, 6 * D]) for l in range(DEPTH)]
    w_in = [din(f"w_in{l}", [D, INC]) for l in range(DEPTH)]
    w_aup = [din(f"w_attn_up{l}", [512, D]) for l in range(DEPTH)]
    w_pool = [din(f"w_pool{l}", [4, 128, 128]) for l in range(DEPTH)]
    w_pup = [din(f"w_pool_up{l}", [512, D]) for l in range(DEPTH)]
    w_out = [din(f"w_out{l}", [D, D]) for l in range(DEPTH)]
    w_g = [din(f"w_ffn_gate{l}", [D, FFN]) for l in range(DEPTH)]
    w_u = [din(f"w_ffn_up{l}", [D, FFN]) for l in range(DEPTH)]
    w_dn = [din(f"w_down_r{l}", [8, 128, NJ * 128]) for l in range(DEPTH)]
    cvec_d = din("cvec", [128, KC * 2])
    bada_d = din("bada", [128, DEPTH * 48])
    gains_d = din("gains", [128, DEPTH * 4 * KC])
    qkg_d = din("qkg", [128, DEPTH * 2])
    pscale_d = din("pscale", [128, DEPTH * 4])
    cos_d = din("cosT", [128, TS])
    sin_d = din("sinT", [128, TS])
    bias_d = din("biasT", [128, 40])
    ck_d = din("cacheK", [DEPTH, 128, 2 * 256])
    cv_d = din("cacheV", [DEPTH, 128, 2 * 2 * 64])
    mt_d = din("mtab", [6, 128, 3072], BF16)
    cm_d = din("cmats", [128, 3 * 128], BF16)
    y_out = dout("yT", [D, T])
    k_out = dout("kT_out", [DEPTH, 128, T])
    v_out = dout("v_out", [DEPTH, T, 128])

    if debug:
        dbg_q = dout("dbg_q", [128, 4 * T], BF16)
        dbg_OT = dout("dbg_OT", [128, 4 * T], BF16)
        dbg_yT = dout("dbg_yT", [128, 4 * T], BF16)
        dbg_mT = dout("dbg_mT", [128, 8 * T], BF16)
        dbg_x1 = dout("dbg_x1", [128, 8 * T], F32)

    P = Prog()
    stack = ExitStack()

    def sb(name, shape, dt):
        return stack.enter_context(nc.sbuf_tensor(name, list(shape), dt))

    xT = sb("xTs", [128, KC * T], F32)
    hT = sb("hTs", [128, KC * T], BF16)
    U = sb("U", [128, 35072], BF16)
    ring = sb("ring", [128, NSLOT * SLOT], BF16)
    fscr = sb("fscr", [128, 8 * CH], F32)
    bscr = sb("bscr", [128, 8 * CH], BF16)
    cosT = sb("cosTs", [128, TS], F32)
    sinT = sb("sinTs", [128, TS], F32)
    biasT = sb("biasTs", [128, 40], F32)
    cvec = sb("cvecs", [128, KC * 2], F32)
    siluc = sb("siluc", [128, KC * 2], BF16)
    bada = sb("badas", [128, DEPTH * 48], F32)
    gains = sb("gainss", [128, DEPTH * 4 * KC], F32)
    qkg = sb("qkgs", [128, DEPTH * 2], F32)
    pscale = sb("pscales", [128, DEPTH * 4], F32)
    cmats = sb("cmatss", [128, 3 * 128], BF16)
    modT = sb("modT", [128, 2 * 96], F32)
    eff = sb("eff", [128, 2 * 96], F32)
    epsT = sb("epsT", [128, 1], F32)
    wplT = sb("wplT", [128, 512], BF16)
    banks = [stack.enter_context(nc.psum_tensor(f"bank{i}", [128, CH], F32)) for i in range(8)]

    B = {}

    def buf(*key):
        if key not in B:
            B[key] = Buf(str(key))
        return B[key]

    def alias(a, b):
        a.aliases.append(b)
        b.aliases.append(a)

    bankb = [buf("bank", i) for i in range(8)]
    slotb = [buf("slot", i) for i in range(NSLOT)]
    fsb = [buf("fscr", i) for i in range(8)]
    bsb = [buf("bscr", i) for i in range(8)]
    cnt = {"bank": 0, "slot": 0, "f": 0, "b": 0, "acc": 0}

    def nacc():
        i = 5 + cnt["acc"] % 2
        cnt["acc"] += 1
        return banks[i], bankb[i]

    wide = {"n": NBANK}

    def nbank():
        i = cnt["bank"] % wide["n"]
        cnt["bank"] += 1
        return banks[i], bankb[i]

    def nslot():
        i = cnt["slot"] % NSLOT
        cnt["slot"] += 1
        return ring[:, i * SLOT:(i + 1) * SLOT], slotb[i], i

    def nf():
        i = cnt["f"] % 8
        cnt["f"] += 1
        return fscr[:, i * CH:(i + 1) * CH], fsb[i]

    def nb():
        i = cnt["b"] % 8
        cnt["b"] += 1
        return bscr[:, i * CH:(i + 1) * CH], bsb[i]

    O_OT, O_YT, O_MT, O_Q, O_DT, O_KT, O_KTD, O_VA = 0, 6144, 12288, 12288, 18432, 24576, 26112, 29696
    KTDW = 1792

    def xv(kc, t0, n):
        return xT[:, kc * T + t0: kc * T + t0 + n]

    def hv(kc, t0, n):
        return hT[:, kc * T + t0: kc * T + t0 + n]

    def uv(off, j, t0, n, p0=0, p1=128):
        return U[p0:p1, off + j * T + t0: off + j * T + t0 + n]

    def ptok(i):
        return U[:, O_OT + i * 512: O_OT + (i + 1) * 512]

    def ktd(g, p0, p1, k0, n):
        return U[p0:p1, O_KTD + g * KTDW + k0: O_KTD + g * KTDW + k0 + n]

    def vaug(tile_, g, c0, n):
        o = O_VA + (tile_ * 2 + g) * 192 + c0
        return U[:, o:o + n]

    def aTv(j, t0, n):
        return U[:, j * T + t0: j * T + t0 + n]

    fs_mix = U[:, 0:8192].bitcast(F32)
    fs_ffn = hT[:, 0:8192].bitcast(F32)

    def bq(j, c):
        return buf("q", j, c)

    def bh(kc, c):
        return buf("h", kc, c)

    def bx(kc, c):
        return buf("x", kc, c)

    mixer_bufs = []

    def mb(*key):
        b = buf(*key)
        if b not in mixer_bufs:
            mixer_bufs.append(b)
        return b

    aT_all = buf("aT_all")

    def reg_alias_all():
        for b in mixer_bufs:
            if aT_all not in b.aliases:
                alias(b, aT_all)

    ones_m = cmats[:, 0:128]
    bd_m = cmats[:, 128:256]
    rot_m = cmats[:, 256:384]

    def dma_load(q, out_ap, in_ap, wbufs, key, rbufs=(), final=False):
        return P.op(q, lambda e, o=out_ap, i=in_ap: e.dma_start(out=o, in_=i), reads=rbufs, writes=wbufs, dma=key, final=final)

    def mm(out_ap, lhsT, rhs, start, stop, reads, writes):
        return P.op("pe", lambda e, o=out_ap, l=lhsT, r=rhs, s=start, t=stop: e.matmul(o, l, r, start=s, stop=t),
                    reads=reads, writes=writes)

    def act(out_ap, in_ap, func, reads, writes, bias=None, scale=None):
        kw = {}
        if bias is not None:
            kw["bias"] = bias
        if scale is not None:
            kw["scale"] = scale
        return P.op("act", lambda e, o=out_ap, i=in_ap, f=func, k=kw: e.activation(o, i, f, **k),
                    reads=reads, writes=writes)

    def dve(fn, reads, writes):
        return P.op("dve", fn, reads=reads, writes=writes)

    def load_slab(src_ap, shape3):
        sl, sbuf_, si = nslot()
        a, b = shape3
        dst = sl[:, 0:a * b].rearrange("p (a b) -> p a b", b=b)
        dma_load("pool", dst, src_ap, [sbuf_], f"slot{si}")
        return dst, sbuf_

    def rstd_from_bank(bank, bbank, scale):
        t1, t1b = nf()
        act(t1, bank[:], AF.Ln, [bbank], [t1b], bias=epsT[:, 0:1], scale=scale)
        t2, t2b = nf()
        act(t2, t1, AF.Exp, [t1b], [t2b], scale=-0.5)
        return t2, t2b

    def norm_mod(l, which, parity, chunks=(0, 1, 2)):
        for c in chunks:
            v = 0 if c < 2 else 1
            t0 = c * CH
            bank, bb = nbank()
            for kc in range(KC):
                sq, sqb = nb()
                act(sq, xv(kc, t0, CH), AF.Square, [bx(kc, c)], [sqb])
                mm(bank[:], ones_m, sq, kc == 0, kc == KC - 1, [sqb], [bb])
            rs, rsb = rstd_from_bank(bank, bb, 1.0 / D)
            for kc in range(KC):
                A = eff[:, parity * 96 + (0 if which == 0 else 2) * 16 + kc * 2 + v: parity * 96 + (0 if which == 0 else 2) * 16 + kc * 2 + v + 1]
                mrow = (0 if which == 0 else 3) * 8 + kc
                sh = modT[:, parity * 96 + mrow * 2 + v: parity * 96 + mrow * 2 + v + 1]
                tmp, tb = nf()
                dve(lambda e, o=tmp, i0=xv(kc, t0, CH), s=A, i1=rs: e.scalar_tensor_tensor(o, i0, s, i1, ALU.mult, ALU.mult),
                    [bx(kc, c), rsb, buf("eff", parity)], [tb])
                act(hv(kc, t0, CH), tmp, AF.Identity, [tb, buf("mod", parity)], [bh(kc, c)], bias=sh)

    def post_norm_residual(l, which, parity, c, fs, fs_bufs, produce):
        v = 0 if c < 2 else 1
        t0 = c * CH
        ssb, ssbb = nacc()
        pend = None
        for dt in range(KC):
            bank, bb = produce(dt)
            if pend is not None:
                mm(ssb[:], ones_m, pend[0], pend[2] == 0, False, [pend[1]], [ssbb])
            act(fs[:, dt * CH:(dt + 1) * CH], bank[:], AF.Copy, [bb], [fs_bufs[dt]])
            sq, sqb = nb()
            act(sq, bank[:], AF.Square, [bb], [sqb])
            pend = (sq, sqb, dt)
        mm(ssb[:], ones_m, pend[0], False, True, [pend[1]], [ssbb])
        rs, rsb = rstd_from_bank(ssb, ssbb, 1.0 / D)
        for dt in range(KC):
            G = eff[:, parity * 96 + (1 if which == 0 else 3) * 16 + dt * 2 + v: parity * 96 + (1 if which == 0 else 3) * 16 + dt * 2 + v + 1]
            tmp, tb = nf()
            dve(lambda e, o=tmp, i0=fs[:, dt * CH:(dt + 1) * CH], i1=rs: e.tensor_tensor(o, i0, i1, ALU.mult),
                [fs_bufs[dt], rsb], [tb])
            dve(lambda e, o=xv(dt, t0, CH), i0=tmp, s=G: e.scalar_tensor_tensor(o, i0, s, o, ALU.mult, ALU.add),
                [tb, buf("eff", parity), bx(dt, c)], [bx(dt, c)])

    def compute_mod(l, between=None):
        parity = l % 2
        mbank, mbb = banks[7], bankb[7]
        for s in range(12):
            src = w_ada[l].rearrange("(k p) c -> p k c", p=128)[:, :, s * 512:(s + 1) * 512]
            w, wb = load_slab(src, (KC, 512))
            for ct in range(4):
                idx = s * 4 + ct
                for kc in range(KC):
                    mm(mbank[:, idx * 2: idx * 2 + 2], w[:, kc, ct * 128:(ct + 1) * 128], siluc[:, kc * 2: kc * 2 + 2],
                       kc == 0, kc == KC - 1, [wb, buf("siluc")], [mbb])
            if between is not None:
                between(s)
        mo = modT[:, parity * 96:(parity + 1) * 96]
        dve(lambda e, o=mo.rearrange("p (a b) -> p a b", b=2), i0=mbank[:, 0:96].rearrange("p (a b) -> p a b", b=2),
            i1=bada[:, l * 48:(l + 1) * 48].unsqueeze(2).broadcast_to([128, 48, 2]): e.tensor_tensor(o, i0, i1, ALU.add),
            [mbb, buf("consts")], [buf("mod", parity)])

        def gv(n):
            return gains[:, (l * 4 + n) * KC:(l * 4 + n + 1) * KC].unsqueeze(2).broadcast_to([128, KC, 2])

        def mv(m):
            return modT[:, parity * 96 + m * 16: parity * 96 + (m + 1) * 16].rearrange("p (a b) -> p a b", b=2)

        def ev(m):
            return eff[:, parity * 96 + m * 16: parity * 96 + (m + 1) * 16].rearrange("p (a b) -> p a b", b=2)

        rb, wbf = [buf("mod", parity), buf("consts")], [buf("eff", parity)]
        dve(lambda e: e.scalar_tensor_tensor(ev(0), mv(1), 1.0, gv(0), ALU.add, ALU.mult), rb, wbf)
        dve(lambda e: e.tensor_tensor(ev(1), mv(2), gv(1), ALU.mult), rb, wbf)
        dve(lambda e: e.scalar_tensor_tensor(ev(2), mv(4), 1.0, gv(2), ALU.add, ALU.mult), rb, wbf)
        dve(lambda e: e.tensor_tensor(ev(3), mv(5), gv(3), ALU.mult), rb, wbf)

    cb = buf("consts")
    for (dst, src) in ((cvec, cvec_d), (bada, bada_d), (gains, gains_d), (qkg, qkg_d), (pscale, pscale_d),
                       (cosT, cos_d), (sinT, sin_d), (biasT, bias_d), (cmats, cm_d)):
        dma_load("sp", dst[:], src[:, :], [cb], "consts")
    dve(lambda e: e.memset(epsT[:], EPS), [], [buf("eps")])
    act(siluc[:], cvec[:], AF.Silu, [cb, buf("eps")], [buf("siluc")])
    for kc in range(KC):
        dma_load("sp", xT[:, kc * T:(kc + 1) * T], x_in[kc * 128:(kc + 1) * 128, :], [bx(kc, c) for c in range(NCH)], f"xin{kc % 2}")
    compute_mod(0)

    for l in range(n_layers):
        P.epoch = 2 * l + 1
        parity = l % 2
        wl_in = w_in[l].rearrange("(k p) c -> p k c", p=128)

        wide["n"] = 8 if l > 0 else 7
        norm_mod(l, 0, parity)

        ckb = mb("ktd_cache")
        for g in range(2):
            dma_load("pool", ktd(g, 0, 128, 0, 256), ck_d[l][:, g * 256:(g + 1) * 256], [ckb], "cachek")
        vab = [mb("vaug", t_) for t_ in range(14)]
        cvsrc = cv_d[l].rearrange("p (t g d) -> p t g d", t=2, g=2)
        for dup in range(2):
            for t_ in range(2):
                dst = U[:, O_VA + t_ * 384: O_VA + (t_ + 1) * 384].rearrange("p (g c) -> p g c", c=192)[:, :, dup * 128: dup * 128 + 64]
                dma_load("pool", dst, cvsrc[:, t_], [vab[t_]], "cachev")
        va_all = U[:, O_VA:O_VA + 14 * 384].rearrange("p (a c) -> p a c", c=192)[:, :, 64:128]
        dve(lambda e, o=va_all: e.memset(o, 1.0), [], vab)

        gq = qkg[:, l * 2: l * 2 + 1]
        gk = qkg[:, l * 2 + 1: l * 2 + 2]
        wsl = {}

        def get_w(name):
            if name not in wsl:
                if name == "q":
                    wsl[name] = load_slab(wl_in[:, :, 0:512], (KC, 512))
                else:
                    wsl[name] = load_slab(wl_in[:, :, 512:768], (KC, 256))
            return wsl[name]

        def qk_unit(is_k, j, c):
            st = {}
            t0 = c * CH
            gain = gk if is_k else gq
            dest = uv(O_KT, 0, t0, CH) if is_k else uv(O_Q, j, t0, CH)
            destb = mb("kT", c) if is_k else mb("q", j, c)

            def A():
                w, wb = get_w("kv" if is_k else "q")
                col0 = 0 if is_k else j * 128
                bank, bb = nbank()
                for kc in range(KC):
                    mm(bank[:], w[:, kc, col0:col0 + 128], hv(kc, t0, CH), kc == 0, kc == KC - 1, [wb, bh(kc, c)], [bb])
                sq, sqb = nb()
                act(sq, bank[:], AF.Square, [bb], [sqb])
                st.update(bank=bank, bb=bb, sq=sq, sqb=sqb)

            def B_():
                bank, bb = st["bank"], st["bb"]
                b2, b2b = nbank()
                mm(b2[:], bd_m, st["sq"], True, True, [st["sqb"], cb], [b2b])
                rs, rsb = rstd_from_bank(b2, b2b, 1.0 / 64)
                if is_k:
                    kf, kfb = nf()
                    dve(lambda e, o=kf, i0=bank[:], s=gain, i1=rs: e.scalar_tensor_tensor(o, i0, s, i1, ALU.mult, ALU.mult),
                        [bb, rsb, cb], [kfb])
                    dma_load("sp", k_out[l][:, t0:t0 + CH], kf, [buf("kout")], "kout", rbufs=[kfb], final=True)
                    if c == 2:
                        act(dest, kf, AF.Copy, [kfb], [destb])
                    else:
                        qn, qnb = nb()
                        act(qn, kf, AF.Copy, [kfb], [qnb])
                        st.update(qn=qn, qnb=qnb)
                else:
                    if c == 2:
                        dve(lambda e, o=dest, i0=bank[:], s=gain, i1=rs: e.scalar_tensor_tensor(o, i0, s, i1, ALU.mult, ALU.mult),
                            [bb, rsb, cb], [destb])
                    else:
                        qn, qnb = nb()
                        dve(lambda e, o=qn, i0=bank[:], s=gain, i1=rs: e.scalar_tensor_tensor(o, i0, s, i1, ALU.mult, ALU.mult),
                            [bb, rsb, cb], [qnb])
                        st.update(qn=qn, qnb=qnb)

            def C():
                if c < 2:
                    qn, qnb = st["qn"], st["qnb"]
                    b3, b3b = nbank()
                    mm(b3[:], rot_m, qn, True, True, [qnb, cb], [b3b])
                    t1, t1b = nf()
                    P.op("pool", lambda e, o=t1, i0=qn, i1=cosT[:, t0:t0 + CH]: e.tensor_tensor(o, i0, i1, ALU.mult), reads=[qnb, cb], writes=[t1b])
                    t2, t2b = nf()
                    dve(lambda e, o=t2, i0=b3[:], i1=sinT[:, t0:t0 + CH]: e.tensor_tensor(o, i0, i1, ALU.mult), [b3b, cb], [t2b])
                    dve(lambda e, o=dest, i0=t1, i1=t2: e.tensor_tensor(o, i0, i1, ALU.add), [t1b, t2b], [destb])
                if is_k:
                    k0 = 256 + t0 if c < 2 else 1280
                    kd = mb("ktd", c)
                    for g in range(2):
                        for half in range(2):
                            o = ktd(g, half * 64, half * 64 + 64, k0, CH)
                            i = uv(O_KT, 0, t0, CH, g * 64, g * 64 + 64)
                            if (g + half) % 2 == 1:
                                dve(lambda e, o=o, i=i: e.tensor_copy(o, i), [destb], [kd])
                            else:
                                act(o, i, AF.Copy, [destb], [kd])
            return [A, B_, C]

        def v_unit(tb4):
            def A():
                wkv, wkvb = get_w("kv")
                bank, bb = nbank()
                for ti in range(4):
                    i = tb4 * 4 + ti
                    for kc in range(KC):
                        mm(bank[:, ti * 128:(ti + 1) * 128], hv(kc, i * 128, 128), wkv[:, kc, 128:256], kc == 0, kc == KC - 1,
                           [wkvb, bh(kc, i // 4)], [bb])
                vs, vsb = nf()
                dve(lambda e, o=vs, i=bank[:]: e.tensor_copy(o, i), [bb], [vsb])
                dma_load("sp", v_out[l][tb4 * 512:(tb4 + 1) * 512, :].rearrange("(t p) c -> p t c", p=128),
                         vs.rearrange("p (t c) -> p t c", c=128), [buf("vout")], "vout", rbufs=[vsb], final=True)
                for dup in range(2):
                    dst = U[:, O_VA + (2 + tb4 * 4) * 384: O_VA + (2 + tb4 * 4 + 4) * 384].rearrange(
                        "p (t g c) -> p t g c", g=2, c=192)[:, :, :, dup * 128: dup * 128 + 64]
                    act(dst, vs.rearrange("p (t g d) -> p t g d", g=2, d=64), AF.Copy, [vsb], [vab[2 + tb4 * 4 + ti] for ti in range(4)])
            return [A]

        units = [qk_unit(False, j, c) for j in range(4) for c in range(NCH)]
        units += [qk_unit(True, 0, c) for c in range(NCH)]
        units += [v_unit(t_) for t_ in range(3)]
        LAG = (0, 1, 3)
        for i in range(len(units) + LAG[2]):
            for stage in range(3):
                u = i - LAG[stage]
                if 0 <= u < len(units) and stage < len(units[u]):
                    units[u][stage]()
        wp, wpb = load_slab(wl_in[:, :, 768:1280], (KC, 512))
        for i in range(12):
            bank, bb = nbank()
            for kc in range(KC):
                mm(bank[:], hv(kc, i * 128, 128), wp[:, kc, :], kc == 0, kc == KC - 1, [wpb, bh(kc, i // 4)], [bb])
            if i % 2 == 0:
                act(ptok(i), bank[:], AF.Copy, [bb], [mb("ptok", i)])
            else:
                dve(lambda e, o=ptok(i), i_=bank[:]: e.tensor_copy(o, i_), [bb], [mb("ptok", i)])

        wpl = wplT[:, :].rearrange("p (g d) -> p g d", d=128)
        wplb = buf("wpl")
        dma_load("pool", wpl, w_pool[l].rearrange("g c d -> c g d"), [wplb], "wpl")
        mslabs = {}
        for c in range(NCH):
            for s in (2 * c, 2 * c + 1):
                sl, sbuf_, si = nslot()
                dma_load("pool", sl[:, 0:3072], mt_d[s], [sbuf_], f"slot{si}")
                mslabs[s] = (sl[:, 0:3072].rearrange("p (t g r o) -> p t g r o", t=2, g=4, r=3), sbuf_)
            for g in range(4):
                bank, bb = nbank()
                for ti in range(4):
                    i = c * 4 + ti
                    rr = [r for r in (-1, 0, 1) if not ((r == -1 and i in (0, 8, 10)) or (r == 1 and i in (7, 9, 11)))]
                    ms, msb = mslabs[i // 2]
                    for n, r in enumerate(rr):
                        mm(bank[:, ti * 128:(ti + 1) * 128], ptok(i + r)[:, g * 128:(g + 1) * 128], ms[:, i % 2, g, r + 1, :],
                           n == 0, n == len(rr) - 1, [msb, mb("ptok", i + r)], [bb])
                act(uv(O_DT, g, c * CH, CH), bank[:], AF.Copy, [bb], [mb("dT", g, c)])
                b2, b2b = nbank()
                mm(b2[:], wpl[:, g, :], uv(O_DT, g, c * CH, CH), True, True, [wplb, mb("dT", g, c)], [b2b])
                dve(lambda e, o=uv(O_YT, g, c * CH, CH), i=b2[:], s=pscale[:, l * 4 + g: l * 4 + g + 1]: e.tensor_scalar(o, i, s, None, ALU.mult),
                    [b2b, cb], [mb("yT", g, c)])

        wide["n"] = NBANK
        LOOK = 3
        tasks = []

        def mk_norm(h, c, ob, obb):
            j, hh = h // 2, h % 2
            p0, p1 = hh * 64, hh * 64 + 64
            d0, d1 = (64, 128) if hh == 0 else (0, 64)

            def fn():
                l1, l1b = nf()
                dve(lambda e, o=l1[p0:p1, :], i=ob[d0:d1, :]: e.tensor_copy(o, i), [obb], [l1b])
                act(l1[p0:p1, :], l1[p0:p1, :], AF.Ln, [l1b], [l1b])
                l2, l2b = nf()
                act(l2[p0:p1, :], l1[p0:p1, :], AF.Exp, [l1b], [l2b], scale=-1.0)
                dve(lambda e, o=uv(O_OT, j, c * CH, CH, p0, p1), i0=ob[p0:p1, :], i1=l2[p0:p1, :]: e.tensor_tensor(o, i0, i1, ALU.mult),
                    [obb, l2b], [mb("OT", j, c)])
            return fn

        for h in range(8):
            g, j, hh = h // 4, h // 2, h % 2
            p0, p1 = hh * 64, hh * 64 + 64
            vc0 = 0 if hh == 0 else 64
            for c in range(NCH):
                st = {}

                nrm = None
                if c < 2:
                    for kt in range(10):
                        def front(kt=kt, c=c, g=g, j=j, p0=p0, p1=p1, st=st):
                            if kt == 0:
                                st["ob"], st["obb"] = nacc()
                            sbk, sbkb = nbank()
                            kb_ = ckb if kt < 2 else mb("ktd", (kt - 2) // 4)
                            mm(sbk[:], ktd(g, p0, p1, kt * 128, 128), uv(O_Q, j, c * CH, CH, p0, p1), True, True,
                               [kb_, mb("q", j, c)], [sbkb])
                            pT, pTb = nb()
                            for half in range(2):
                                bi = (kt * 2 + c) * 2 + half
                                act(pT[:, half * 256:(half + 1) * 256], sbk[:, half * 256:(half + 1) * 256], AF.Exp, [sbkb, cb], [pTb],
                                    bias=biasT[:, bi:bi + 1], scale=0.125)
                            st[kt] = (pT, pTb)

                        def back(kt=kt, c=c, h=h, g=g, vc0=vc0, st=st):
                            pT, pTb = st[kt]
                            mm(st["ob"][:], vaug(kt, g, vc0, 128), pT, kt == 0, kt == 9, [vab[kt], pTb], [st["obb"]])
                            if kt == 9:
                                mk_norm(h, c, st["ob"], st["obb"])()
                        tasks.append((front, back))
                else:
                    for s_ in range(2):
                        def front(s_=s_, g=g, j=j, p0=p0, p1=p1, st=st):
                            if s_ == 0:
                                st["ob"], st["obb"] = nacc()
                            sbk, sbkb = nbank()
                            pT, pTb = nb()
                            for kt in range(2):
                                mm(sbk[:, kt * 256:(kt + 1) * 256], ktd(g, p0, p1, 1280 + s_ * 256 + kt * 128, 128),
                                   uv(O_Q, j, 1024 + s_ * 256, 256, p0, p1), True, True, [mb("ktd", 2), mb("q", j, 2)], [sbkb])
                            act(pT, sbk[:], AF.Exp, [sbkb], [pTb], scale=0.125)
                            st[s_] = (pT, pTb)

                        def back(s_=s_, h=h, g=g, vc0=vc0, st=st):
                            pT, pTb = st[s_]
                            for kt in range(2):
                                mm(st["ob"][:, s_ * 256:(s_ + 1) * 256], vaug(10 + s_ * 2 + kt, g, vc0, 128), pT[:, kt * 256:(kt + 1) * 256],
                                   kt == 0, kt == 1, [vab[10 + s_ * 2 + kt], pTb], [st["obb"]])
                            if s_ == 1:
                                mk_norm(h, 2, st["ob"], st["obb"])()
                        tasks.append((front, back))
        for i in range(len(tasks) + LOOK):
            if i < len(tasks):
                tasks[i][0]()
            if i >= LOOK:
                tasks[i - LOOK][1]()

        if debug and l == 0:
            P.op("sp", lambda e: e.dma_start(out=dbg_q[:, :], in_=U[:, O_Q:O_Q + 4 * T]), reads=[mb("q", j, c) for j in range(4) for c in range(NCH)], writes=[buf("dbgo")], dma="dbg", final=True)
            P.op("sp", lambda e: e.dma_start(out=dbg_OT[:, :], in_=U[:, O_OT:O_OT + 4 * T]), reads=[mb("OT", j, c) for j in range(4) for c in range(NCH)], writes=[buf("dbgo")], dma="dbg", final=True)
            P.op("sp", lambda e: e.dma_start(out=dbg_yT[:, :], in_=U[:, O_YT:O_YT + 4 * T]), reads=[mb("yT", j, c) for j in range(4) for c in range(NCH)], writes=[buf("dbgo")], dma="dbg", final=True)
        wide["n"] = 7
        waul = w_aup[l].rearrange("(k p) c -> p k c", p=128)
        wpul = w_pup[l].rearrange("(k p) c -> p k c", p=128)
        for dt in range(KC):
            sl, sbuf_, si = nslot()
            wgd = sl[:, 0:2048].rearrange("p (k a c) -> p k a c", a=2, c=128)
            wga = wgd[:, :, 0, :]
            wgp = wgd[:, :, 1, :]
            wau = sl[:, 2048:2560].rearrange("p (k c) -> p k c", c=128)
            wpu = sl[:, 2560:3072].rearrange("p (k c) -> p k c", c=128)
            waub = wpub = sbuf_
            dma_load("pool", wga, wl_in[:, :, 1280 + dt * 128: 1280 + (dt + 1) * 128], [sbuf_], f"slot{si}")
            dma_load("pool", wgp, wl_in[:, :, 2304 + dt * 128: 2304 + (dt + 1) * 128], [sbuf_], f"slot{si}")
            dma_load("pool", wau, waul[:, :, dt * 128:(dt + 1) * 128], [sbuf_], f"slot{si}")
            dma_load("pool", wpu, wpul[:, :, dt * 128:(dt + 1) * 128], [sbuf_], f"slot{si}")
            for c in range(NCH):
                t0 = c * CH
                ba, bab = nbank()
                for kc in range(4):
                    mm(ba[:], wau[:, kc, :], uv(O_OT, kc, t0, CH), kc == 0, kc == 3, [waub, mb("OT", kc, c)], [bab])
                bp, bpb = nbank()
                for kc in range(4):
                    mm(bp[:], wpu[:, kc, :], uv(O_YT, kc, t0, CH), kc == 0, kc == 3, [wpub, mb("yT", kc, c)], [bpb])
                bga, bgab = nbank()
                for kc in range(KC):
                    mm(bga[:], wga[:, kc, :], hv(kc, t0, CH), kc == 0, kc == KC - 1, [sbuf_, bh(kc, c)], [bgab])
                bgp, bgpb = nbank()
                for kc in range(KC):
                    mm(bgp[:], wgp[:, kc, :], hv(kc, t0, CH), kc == 0, kc == KC - 1, [sbuf_, bh(kc, c)], [bgpb])
                sa, sab = nf()
                act(sa, bga[:], AF.Sigmoid, [bgab], [sab])
                sp_, spb = nf()
                act(sp_, bgp[:], AF.Sigmoid, [bgpb], [spb])
                m1, m1b = nf()
                dve(lambda e, o=m1, i0=ba[:], i1=sa: e.tensor_tensor(o, i0, i1, ALU.mult), [bab, sab], [m1b])
                m2, m2b = nf()
                dve(lambda e, o=m2, i0=bp[:], i1=sp_: e.tensor_tensor(o, i0, i1, ALU.mult), [bpb, spb], [m2b])
                dve(lambda e, o=uv(O_MT, dt, t0, CH), i0=m1, i1=m2: e.tensor_tensor(o, i0, i1, ALU.add), [m1b, m2b], [mb("mT", dt, c)])
        fsm_bufs = [mb("fsmix", dt) for dt in range(KC)]
        if l == 0:
            for dt in range(KC):
                for c in range(NCH):
                    for j in range(4):
                        alias(mb("mT", dt, c), mb("q", j, c)) if dt < 4 and j == dt else None
                        alias(mb("mT", dt, c), mb("dT", j, c)) if dt >= 4 and j == dt - 4 else None
                for j in range(4):
                    for c in range(NCH):
                        alias(fsm_bufs[dt], mb("OT", j, c))
                        alias(fsm_bufs[dt], mb("yT", j, c))
                for i in range(12):
                    alias(fsm_bufs[dt], mb("ptok", i))
            for i in range(12):
                for j in range(4):
                    for c in range(NCH):
                        alias(mb("ptok", i), mb("OT", j, c))
            reg_alias_all()

        if debug and l == 0:
            P.op("sp", lambda e: e.dma_start(out=dbg_mT[:, :], in_=U[:, O_MT:O_MT + 8 * T]), reads=[mb("mT", j, c) for j in range(8) for c in range(NCH)], writes=[buf("dbgo")], dma="dbg", final=True)
        wide["n"] = NBANK
        wo0, wo0b = load_slab(w_out[l].rearrange("(k p) c -> p k c", p=128)[:, :, 0:512], (KC, 512))
        wo1, wo1b = load_slab(w_out[l].rearrange("(k p) c -> p k c", p=128)[:, :, 512:1024], (KC, 512))
        for c in range(NCH):
            def prod(dt, c=c):
                bank, bb = nbank()
                w, wb = (wo0, wo0b) if dt < 4 else (wo1, wo1b)
                for kc in range(KC):
                    mm(bank[:], w[:, kc, (dt % 4) * 128:(dt % 4 + 1) * 128], uv(O_MT, kc, c * CH, CH), kc == 0, kc == KC - 1,
                       [wb, mb("mT", kc, c)], [bb])
                return bank, bb
            post_norm_residual(l, 0, parity, c, fs_mix, fsm_bufs, prod)
            if c >= 1:
                norm_mod(l, 1, parity, chunks=(c - 1,))

        if debug and l == 0:
            P.op("sp", lambda e: e.dma_start(out=dbg_x1[:, :], in_=xT[:, :]), reads=[bx(kc, c) for kc in range(KC) for c in range(NCH)], writes=[buf("dbgo")], dma="dbg", final=True)
        P.epoch = 2 * l + 2
        norm_mod(l, 1, parity, chunks=(2,))
        nxt_mod = (l + 1 < n_layers)
        ada_state = {"s": 0}

        def ada_slab(lnext):
            s = ada_state["s"]
            if s >= 12:
                return
            ada_state["s"] += 1
            src = w_ada[lnext].rearrange("(k p) c -> p k c", p=128)[:, :, s * 512:(s + 1) * 512]
            w, wb = load_slab(src, (KC, 512))
            for ct in range(4):
                idx = s * 4 + ct
                for kc in range(KC):
                    mm(banks[7][:, idx * 2: idx * 2 + 2], w[:, kc, ct * 128:(ct + 1) * 128], siluc[:, kc * 2: kc * 2 + 2],
                       kc == 0, kc == KC - 1, [wb, buf("siluc")], [bankb[7]])

        wgl = w_g[l].rearrange("(k p) c -> p k c", p=128)
        wul = w_u[l].rearrange("(k p) c -> p k c", p=128)
        aTb = [[buf("aT", j, c) for c in range(NCH)] for j in range(NJ)]
        if l == 0:
            for j in range(NJ):
                for c in range(NCH):
                    alias(aTb[j][c], aT_all)
        wide["n"] = 7
        for jb in range(NJ // 2):
            sl, sbuf_, si = nslot()
            wgu = sl[:, 0:4096].rearrange("p (k a c) -> p k a c", a=2, c=256)
            dma_load("pool", wgu[:, :, 0, :], wgl[:, :, jb * 256:(jb + 1) * 256], [sbuf_], f"slot{si}")
            dma_load("pool", wgu[:, :, 1, :], wul[:, :, jb * 256:(jb + 1) * 256], [sbuf_], f"slot{si}")
            for jj in range(2):
                j = jb * 2 + jj
                for c in range(NCH):
                    t0 = c * CH
                    bg, bgb = nbank()
                    for kc in range(KC):
                        mm(bg[:], wgu[:, kc, 0, jj * 128:(jj + 1) * 128], hv(kc, t0, CH), kc == 0, kc == KC - 1, [sbuf_, bh(kc, c)], [bgb])
                    bu, bub = nbank()
                    for kc in range(KC):
                        mm(bu[:], wgu[:, kc, 1, jj * 128:(jj + 1) * 128], hv(kc, t0, CH), kc == 0, kc == KC - 1, [sbuf_, bh(kc, c)], [bub])
                    sg, sgb = nf()
                    act(sg, bg[:], AF.Silu, [bgb], [sgb])
                    dve(lambda e, o=aTv(j, t0, CH), i0=sg, i1=bu[:]: e.tensor_tensor(o, i0, i1, ALU.mult), [sgb, bub], [aTb[j][c], aT_all])
            if nxt_mod and jb % 2 == 1:
                ada_slab(l + 1)
        wide["n"] = NBANK
        fsf_bufs = [buf("fsffn", dt) for dt in range(KC)]
        if l == 0:
            for dt in range(KC):
                for kc in range(KC):
                    for c in range(NCH):
                        alias(fsf_bufs[dt], bh(kc, c))
        for c in range(NCH):
            def prodf(dt, c=c):
                w, wb = load_slab(w_dn[l][dt].rearrange("p (j c) -> p j c", c=128), (NJ, 128))
                bank, bb = nbank()
                for j in range(NJ):
                    mm(bank[:], w[:, j, :], aTv(j, c * CH, CH), j == 0, j == NJ - 1, [wb, aTb[j][c], aT_all], [bb])
                if nxt_mod and dt % 2 == 1:
                    ada_slab(l + 1)
                return bank, bb
            post_norm_residual(l, 1, parity, c, fs_ffn, fsf_bufs, prodf)
        if nxt_mod:
            while ada_state["s"] < 12:
                ada_slab(l + 1)
            parity_n = (l + 1) % 2
            mo = modT[:, parity_n * 96:(parity_n + 1) * 96]
            ln = l + 1
            dve(lambda e, o=mo.rearrange("p (a b) -> p a b", b=2), i0=banks[7][:, 0:96].rearrange("p (a b) -> p a b", b=2),
                i1=bada[:, ln * 48:(ln + 1) * 48].unsqueeze(2).broadcast_to([128, 48, 2]): e.tensor_tensor(o, i0, i1, ALU.add),
                [bankb[7], cb], [buf("mod", parity_n)])

            def gv(n, ln=ln):
                return gains[:, (ln * 4 + n) * KC:(ln * 4 + n + 1) * KC].unsqueeze(2).broadcast_to([128, KC, 2])

            def mv(m, pn=parity_n):
                return modT[:, pn * 96 + m * 16: pn * 96 + (m + 1) * 16].rearrange("p (a b) -> p a b", b=2)

            def ev(m, pn=parity_n):
                return eff[:, pn * 96 + m * 16: pn * 96 + (m + 1) * 16].rearrange("p (a b) -> p a b", b=2)

            rb, wbf = [buf("mod", parity_n), cb], [buf("eff", parity_n)]
            dve(lambda e, ev=ev, mv=mv, gv=gv: e.scalar_tensor_tensor(ev(0), mv(1), 1.0, gv(0), ALU.add, ALU.mult), rb, wbf)
            dve(lambda e, ev=ev, mv=mv, gv=gv: e.tensor_tensor(ev(1), mv(2), gv(1), ALU.mult), rb, wbf)
            dve(lambda e, ev=ev, mv=mv, gv=gv: e.scalar_tensor_tensor(ev(2), mv(4), 1.0, gv(2), ALU.add, ALU.mult), rb, wbf)
            dve(lambda e, ev=ev, mv=mv, gv=gv: e.tensor_tensor(ev(3), mv(5), gv(3), ALU.mult), rb, wbf)

    for kc in range(KC):
        P.op("sp", lambda e, kc=kc: e.dma_start(out=y_out[kc * 128:(kc + 1) * 128, :], in_=xT[:, kc * T:(kc + 1) * T]),
             reads=[bx(kc, c) for c in range(NCH)], writes=[buf("yout")], dma="yout", final=True)

    block = stack.enter_context(nc.Block())
    run, _, _ = P.emit(nc, block, stack)

    @block.tensor
    def _(e):
        run("pe", e)

    @block.scalar
    def _(e):
        run("act", e)

    @block.vector
    def _(e):
        run("dve", e)

    @block.gpsimd
    def _(e):
        run("pool", e)

    @block.sync
    def _(e):
        run("sp", e)

    stack.close()
    return nc


POOL_WINDOWS = (2, 4, 8, 16)


def _pool_band(n_seq_tiles_list):
    ntiles = sum(n_seq_tiles_list)
    M = np.zeros((ntiles, 4, 3, 128, 128), np.float32)
    tile0 = 0
    for nt in n_seq_tiles_list:
        n = nt * 128
        for g, w in enumerate(POOL_WINDOWS):
            left = w // 2
            right = w - 1 - left
            for t in range(n):
                lo = max(t - left, 0)
                hi = min(t + right + 1, n)
                val = 1.0 / (hi - lo)
                ti, to = divmod(t, 128)
                for tp in range(lo, hi):
                    tpi, tpo = divmod(tp, 128)
                    M[tile0 + ti, g, tpi - ti + 1, tpo, to] += val
                M[tile0 + ti, g, 1, to, to] -= 1.0
        tile0 += nt
    return M


def _prep(inputs):
    f32 = np.float32
    x_prompt = np.asarray(inputs["x_prompt"], f32)
    x_sample = np.asarray(inputs["x_sample"], f32)
    cache_k = np.asarray(inputs["cache_k"], f32)
    cache_v = np.asarray(inputs["cache_v"], f32)
    c = np.asarray(inputs["c"], f32)
    c_ctx = np.asarray(inputs["c_ctx"], f32)

    shared = {}
    for k in ("w_ada", "w_in", "w_attn_up", "w_pool", "w_pool_up", "w_out", "w_ffn_gate", "w_ffn_up"):
        a = np.asarray(inputs[k], f32)
        for l in range(DEPTH):
            shared[f"{k}{l}"] = np.ascontiguousarray(a[l])
    wd = np.asarray(inputs["w_ffn_down"], f32)
    wdr = wd.reshape(DEPTH, NJ, 128, 8, 128).transpose(0, 3, 2, 1, 4).reshape(DEPTH, 8, 128, NJ * 128)
    for l in range(DEPTH):
        shared[f"w_down_r{l}"] = np.ascontiguousarray(wdr[l])
    b_ada = np.asarray(inputs["b_ada"], f32)
    shared["bada"] = np.ascontiguousarray(b_ada.reshape(DEPTH, 48, 128).transpose(2, 0, 1).reshape(128, DEPTH * 48))
    gs = np.stack([np.asarray(inputs[k], f32) for k in ("n_pre_mix", "n_post_mix", "n_pre_ffn", "n_post_ffn")], 1)
    shared["gains"] = np.ascontiguousarray(gs.reshape(DEPTH, 4, KC, 128).transpose(3, 0, 1, 2).reshape(128, DEPTH * 4 * KC))
    qn = np.asarray(inputs["q_norm"], f32)
    kn = np.asarray(inputs["k_norm"], f32)
    qk = np.stack([qn, kn], 1)
    shared["qkg"] = np.ascontiguousarray(np.tile(qk.transpose(2, 0, 1), (2, 1, 1)).reshape(128, DEPTH * 2))
    ps = np.asarray(inputs["pool_scale"], f32)
    shared["pscale"] = np.ascontiguousarray(ps.reshape(DEPTH, 4, 128).transpose(2, 0, 1).reshape(128, DEPTH * 4))
    ones = np.ones((128, 128), f32)
    bd = np.zeros((128, 128), f32)
    bd[:64, :64] = 1
    bd[64:, 64:] = 1
    rot = np.zeros((128, 128), f32)
    for hb in (0, 64):
        for d in range(32):
            rot[hb + d + 32, hb + d] = -1.0
            rot[hb + d, hb + d + 32] = 1.0
    shared["cmats"] = np.concatenate([ones, bd, rot], 1).astype(ml_dtypes.bfloat16)

    n = TS
    rows = n // 64
    row = np.repeat(np.arange(rows, dtype=f32), 64)
    col = np.tile(np.arange(64, dtype=f32), rows)
    inv = (f32(10000.0) ** (-np.arange(16, dtype=f32) / f32(16))).astype(f32)
    ang = np.concatenate([row[:, None] * inv[None, :], col[:, None] * inv[None, :]], -1).astype(f32)
    cos_s = np.cos(ang).astype(f32)
    sin_s = np.sin(ang).astype(f32)
    idx = np.arange(128) % 32
    cos_sample = np.ascontiguousarray(cos_s[:, idx].T)
    sin_sample = np.ascontiguousarray(sin_s[:, idx].T)
    cos_prompt = np.ones((128, TS), f32)
    sin_prompt = np.zeros((128, TS), f32)

    bias_sample = np.zeros((128, 40), f32)
    bias_prompt = np.full((128, 40), NEG, f32)
    for kt in range(2, 10):
        for cc in range(2):
            for half in range(2):
                if (kt - 2) // 2 == cc * 2 + half:
                    bias_prompt[:, (kt * 2 + cc) * 2 + half] = 0.0

    M_sample = np.concatenate([_pool_band([8]), _pool_band([2, 2])], 0)
    M_prompt = _pool_band([2] * 6)

    def mt_layout(M):
        return np.ascontiguousarray(M.reshape(6, 2, 4, 3, 128, 128).transpose(0, 4, 1, 2, 3, 5).reshape(6, 128, 3072)).astype(ml_dtypes.bfloat16)

    mt_sample = mt_layout(M_sample)
    mt_prompt = mt_layout(M_prompt)

    in_maps = []
    assign = []
    for core in range(8):
        m = dict(shared)
        if core < 4:
            pS = None
            pP = [2 * core, 2 * core + 1]
            xs = np.concatenate([x_sample[core]] + [x_prompt[b] for b in pP], 0)
            cS = c[core]
            m["cosT"], m["sinT"], m["biasT"], m["mtab"] = cos_sample, sin_sample, bias_sample, mt_sample
            ck = cache_k[core]
            ckt = ck.transpose(0, 2, 3, 1)
            ckd = np.concatenate([ckt, ckt], 2)
            m["cacheK"] = np.ascontiguousarray(ckd.transpose(0, 2, 1, 3).reshape(DEPTH, 128, 512))
            cv = cache_v[core]
            m["cacheV"] = np.ascontiguousarray(cv.reshape(DEPTH, 2, 128, 2, 64).transpose(0, 2, 1, 3, 4).reshape(DEPTH, 128, 256))
        else:
            base = 8 + 6 * (core - 4)
            pS = [base + i for i in range(4)]
            pP = [base + 4, base + 5]
            xs = np.concatenate([x_prompt[b] for b in pS + pP], 0)
            cS = c_ctx
            m["cosT"], m["sinT"], m["biasT"], m["mtab"] = cos_prompt, sin_prompt, bias_prompt, mt_prompt
            m["cacheK"] = np.zeros((DEPTH, 128, 512), f32)
            m["cacheV"] = np.zeros((DEPTH, 128, 256), f32)
        m["xT"] = np.ascontiguousarray(xs.T)
        cv2 = np.stack([cS, c_ctx], 1)
        m["cvec"] = np.ascontiguousarray(cv2.reshape(KC, 128, 2).transpose(1, 0, 2).reshape(128, KC * 2))
        in_maps.append(m)
        assign.append((pS, pP))
    return in_maps, assign


_NC_CACHE = {}


def kernel(**inputs):
    in_maps, assign = _prep(inputs)
    if "nc" not in _NC_CACHE:
        _NC_CACHE["nc"] = build_nc()
    nc = _NC_CACHE["nc"]
    res = run_bass_kernel_spmd(nc, in_maps, core_ids=list(range(8)))
    y_prompt = np.zeros((32, 256, D), np.float32)
    y_sample = np.zeros((4, 1024, D), np.float32)
    new_k = np.zeros((32, DEPTH, 256, 2, 64), np.float32)
    new_v = np.zeros((32, DEPTH, 256, 2, 64), np.float32)
    for core in range(8):
        r = res.results[core]
        y = np.asarray(r["yT"]).T
        kt = np.asarray(r["kT_out"])
        vt = np.asarray(r["v_out"])
        pS, pP = assign[core]
        segs = []
        if pS is None:
            y_sample[core] = y[:TS]
        else:
            segs += [(b, i * 256) for i, b in enumerate(pS)]
        segs += [(b, TS + i * 256) for i, b in enumerate(pP)]
        for b, t0 in segs:
            y_prompt[b] = y[t0:t0 + 256]
            new_k[b] = kt[:, :, t0:t0 + 256].transpose(0, 2, 1).reshape(DEPTH, 256, 2, 64)
            new_v[b] = vt[:, t0:t0 + 256, :].reshape(DEPTH, 256, 2, 64)
    return (y_prompt, y_sample, new_k, new_v)
```

```python
import numpy as np
import ml_dtypes
import concourse.bass as bass
import concourse.mybir as mybir
from concourse.bass_utils import run_bass_kernel_spmd

F32 = mybir.dt.float32
BF16 = mybir.dt.bfloat16
AF = mybir.ActivationFunctionType
ALU = mybir.AluOpType

D = 1024
T = 1536
TS = 1024
NCH = 3
CH = 512
KC = 8
DEPTH = 4
FFN = 2816
NJ = 22
INC = 3328
EPS = 1e-6
NEG = -30000.0
SLOT = 4096
NSLOT = 3
NBANK = 5


class Buf:
    __slots__ = ("name", "w", "r", "aliases")

    def __init__(self, name):
        self.name = name
        self.w = None
        self.r = {}
        self.aliases = []


class Op:
    __slots__ = ("eng", "fn", "deps", "sig", "seq", "epoch", "sigidx", "dkey", "dval", "_waw")

    def __init__(self, eng, fn):
        self.eng = eng
        self.fn = fn
        self.deps = []
        self.sig = False
        self.dkey = None
        self.dval = 0
        self._waw = False


class Prog:
    ENGS = ("pe", "act", "dve", "pool", "sp")

    def __init__(self):
        self.ops = {e: [] for e in self.ENGS}
        self.epoch = 0
        self.dma_count = {}
        self.final_waits = []

    def _dep(self, op, p):
        if p is None or p is op:
            return
        if p.dkey is None and p.eng == op.eng:
            return
        if p.dkey is not None and p.dkey == op.dkey and p.eng == op.eng and op._waw:
            return
        op.deps.append(p)
        p.sig = True

    def op(self, eng, fn, reads=(), writes=(), dma=None, final=False):
        o = Op(eng, fn)
        o.epoch = self.epoch
        o.seq = len(self.ops[eng])
        if dma is not None:
            dma = f"{dma}_e{self.epoch}"
            o.dkey = dma
            self.dma_count[dma] = self.dma_count.get(dma, 0) + 16
            o.dval = self.dma_count[dma]
        for b in reads:
            self._dep(o, b.w)
            for a in b.aliases:
                self._dep(o, a.w)
        for b in writes:
            for bb in [b] + b.aliases:
                o._waw = True
                self._dep(o, bb.w)
                o._waw = False
                for r in bb.r.values():
                    self._dep(o, r)
        for b in reads:
            b.r[eng if dma is None else ("dma", id(o))] = o
        for b in writes:
            b.w = o
            b.r = {}
            for a in b.aliases:
                a.r = {}
        self.ops[eng].append(o)
        if final:
            self.final_waits.append(o)
        return o

    def emit(self, nc, block, stack):
        engsem = {}
        dmasem = {}
        for e in self.ENGS:
            cnt = {}
            for o in self.ops[e]:
                if o.dkey is None and o.sig:
                    cnt[o.epoch] = cnt.get(o.epoch, 0) + 1
                    o.sigidx = cnt[o.epoch]
                    if (e, o.epoch) not in engsem:
                        engsem[(e, o.epoch)] = stack.enter_context(nc.semaphore(f"s_{e}_{o.epoch}"))
        for k in self.dma_count:
            dmasem[k] = stack.enter_context(nc.semaphore(f"d_{k}"))

        def run(e, eng):
            waited = {}
            dwaited = {}
            for o in self.ops[e]:
                for p in o.deps:
                    if p.dkey is not None:
                        if dwaited.get(p.dkey, 0) >= p.dval:
                            continue
                        dwaited[p.dkey] = p.dval
                        eng.wait_ge(dmasem[p.dkey], p.dval)
                    else:
                        if waited.get(p.eng, -1) >= p.seq:
                            continue
                        waited[p.eng] = p.seq
                        eng.wait_ge(engsem[(p.eng, p.epoch)], p.sigidx)
                ins = o.fn(eng)
                if o.dkey is not None:
                    ins.then_inc(dmasem[o.dkey], 16)
                elif o.sig:
                    ins.then_inc(engsem[(e, o.epoch)], 1)
            if e == "sp":
                for k in sorted({o.dkey for o in self.final_waits}):
                    eng.wait_ge(dmasem[k], self.dma_count[k])

        return run, engsem, dmasem


def build_nc(n_layers=DEPTH, debug=False):
    from contextlib import ExitStack

    nc = bass.Bass("TRN2", target_bir_lowering=False)
    dr = {}

    def din(name, shape, dt=F32):
        dr[name] = nc.dram_tensor(name, list(shape), dt, kind="ExternalInput").ap()
        return dr[name]

    def dout(name, shape, dt=F32):
        dr[name] = nc.dram_tensor(name, list(shape), dt, kind="ExternalOutput").ap()
        return dr[name]

    x_in = din("xT", [D, T])
    w_ada = [din(f"w_ada{l}", [D, 6 * D]) for l in range(DEPTH)]
    w_in = [din(f"w_in{l}", [D, INC]) for l in range(DEPTH)]
    w_aup = [din(f"w_attn_up{l}", [512, D]) for l in range(DEPTH)]
    w_pool = [din(f"w_pool{l}", [4, 128, 128]) for l in range(DEPTH)]
    w_pup = [din(f"w_pool_up{l}", [512, D]) for l in range(DEPTH)]
    w_out = [din(f"w_out{l}", [D, D]) for l in range(DEPTH)]
    w_g = [din(f"w_ffn_gate{l}", [D, FFN]) for l in range(DEPTH)]
    w_u = [din(f"w_ffn_up{l}", [D, FFN]) for l in range(DEPTH)]
    w_dn = [din(f"w_down_r{l}", [8, 128, NJ * 128]) for l in range(DEPTH)]
    cvec_d = din("cvec", [128, KC * 2])
    bada_d = din("bada", [128, DEPTH * 48])
    gains_d = din("gains", [128, DEPTH * 4 * KC])
    qkg_d = din("qkg", [128, DEPTH * 2])
    pscale_d = din("pscale", [128, DEPTH * 4])
    cos_d = din("cosT", [128, TS])
    sin_d = din("sinT", [128, TS])
    bias_d = din("biasT", [128, 40])
    ck_d = din("cacheK", [DEPTH, 128, 2 * 256])
    cv_d = din("cacheV", [DEPTH, 128, 2 * 2 * 64])
    mt_d = din("mtab", [6, 128, 3072], BF16)
    cm_d = din("cmats", [128, 3 * 128], BF16)
    y_out = dout("yT", [D, T])
    k_out = dout("kT_out", [DEPTH, 128, T])
    v_out = dout("v_out", [DEPTH, T, 128])

    if debug:
        dbg_q = dout("dbg_q", [128, 4 * T], BF16)
        dbg_OT = dout("dbg_OT", [128, 4 * T], BF16)
        dbg_yT = dout("dbg_yT", [128, 4 * T], BF16)
        dbg_mT = dout("dbg_mT", [128, 8 * T], BF16)
        dbg_x1 = dout("dbg_x1", [128, 8 * T], F32)

    P = Prog()
    stack = ExitStack()

    def sb(name, shape, dt):
        return stack.enter_context(nc.sbuf_tensor(name, list(shape), dt))

    xT = sb("xTs", [128, KC * T], F32)
    hT = sb("hTs", [128, KC * T], BF16)
    U = sb("U", [128, 35072], BF16)
    ring = sb("ring", [128, NSLOT * SLOT], BF16)
    fscr = sb("fscr", [128, 8 * CH], F32)
    bscr = sb("bscr", [128, 8 * CH], BF16)
    cosT = sb("cosTs", [128, TS], F32)
    sinT = sb("sinTs", [128, TS], F32)
    biasT = sb("biasTs", [128, 40], F32)
    cvec = sb("cvecs", [128, KC * 2], F32)
    siluc = sb("siluc", [128, KC * 2], BF16)
    bada = sb("badas", [128, DEPTH * 48], F32)
    gains = sb("gainss", [128, DEPTH * 4 * KC], F32)
    qkg = sb("qkgs", [128, DEPTH * 2], F32)
    pscale = sb("pscales", [128, DEPTH * 4], F32)
    cmats = sb("cmatss", [128, 3 * 128], BF16)
    modT = sb("modT", [128, 2 * 96], F32)
    eff = sb("eff", [128, 2 * 96], F32)
    epsT = sb("epsT", [128, 1], F32)
    wplT = sb("wplT", [128, 512], BF16)
    banks = [stack.enter_context(nc.psum_tensor(f"bank{i}", [128, CH], F32)) for i in range(8)]

    B = {}

    def buf(*key):
        if key not in B:
            B[key] = Buf(str(key))
        return B[key]

    def alias(a, b):
        a.aliases.append(b)
        b.aliases.append(a)

    bankb = [buf("bank", i) for i in range(8)]
    slotb = [buf("slot", i) for i in range(NSLOT)]
    fsb = [buf("fscr", i) for i in range(8)]
    bsb = [buf("bscr", i) for i in range(8)]
    cnt = {"bank": 0, "slot": 0, "f": 0, "b": 0, "acc": 0}

    def nacc():
        i = 5 + cnt["acc"] % 2
        cnt["acc"] += 1
        return banks[i], bankb[i]

    wide = {"n": NBANK}

    def nbank():
        i = cnt["bank"] % wide["n"]
        cnt["bank"] += 1
        return banks[i], bankb[i]

    def nslot():
        i = cnt["slot"] % NSLOT
        cnt["slot"] += 1
        return ring[:, i * SLOT:(i + 1) * SLOT], slotb[i], i

    def nf():
        i = cnt["f"] % 8
        cnt["f"] += 1
        return fscr[:, i * CH:(i + 1) * CH], fsb[i]

    def nb():
        i = cnt["b"] % 8
        cnt["b"] += 1
        return bscr[:, i * CH:(i + 1) * CH], bsb[i]

    O_OT, O_YT, O_MT, O_Q, O_DT, O_KT, O_KTD, O_VA = 0, 6144, 12288, 12288, 18432, 24576, 26112, 29696
    KTDW = 1792

    def xv(kc, t0, n):
        return xT[:, kc * T + t0: kc * T + t0 + n]

    def hv(kc, t0, n):
        c_ = t0 // CH
        o = c_ * (KC * CH) + kc * CH + (t0 - c_ * CH)
        return hT[:, o:o + n]

    def uv(off, j, t0, n, p0=0, p1=128):
        return U[p0:p1, off + j * T + t0: off + j * T + t0 + n]

    def ptok(i):
        return U[:, O_OT + i * 512: O_OT + (i + 1) * 512]

    def ktd(g, p0, p1, k0, n):
        return U[p0:p1, O_KTD + g * KTDW + k0: O_KTD + g * KTDW + k0 + n]

    def vaug(tile_, g, c0, n):
        o = O_VA + (tile_ * 2 + g) * 192 + c0
        return U[:, o:o + n]

    def aTv(j, t0, n):
        return U[:, j * T + t0: j * T + t0 + n]

    fs_mix = U[:, 0:8192].bitcast(F32)
    fs_ffn = hT[:, 4096:12288].bitcast(F32)

    def bq(j, c):
        return buf("q", j, c)

    def bh(kc, c):
        return buf("h", kc, c)

    def bx(kc, c):
        return buf("x", kc, c)

    mixer_bufs = []

    def mb(*key):
        b = buf(*key)
        if b not in mixer_bufs:
            mixer_bufs.append(b)
        return b

    aT_all = buf("aT_all")

    def reg_alias_all():
        for b in mixer_bufs:
            if aT_all not in b.aliases:
                alias(b, aT_all)

    ones_m = cmats[:, 0:128]
    bd_m = cmats[:, 128:256]
    rot_m = cmats[:, 256:384]

    def dma_load(q, out_ap, in_ap, wbufs, key, rbufs=(), final=False):
        return P.op(q, lambda e, o=out_ap, i=in_ap: e.dma_start(out=o, in_=i), reads=rbufs, writes=wbufs, dma=key, final=final)

    def mm(out_ap, lhsT, rhs, start, stop, reads, writes):
        return P.op("pe", lambda e, o=out_ap, l=lhsT, r=rhs, s=start, t=stop: e.matmul(o, l, r, start=s, stop=t),
                    reads=reads, writes=writes)

    def act(out_ap, in_ap, func, reads, writes, bias=None, scale=None):
        kw = {}
        if bias is not None:
            kw["bias"] = bias
        if scale is not None:
            kw["scale"] = scale
        return P.op("act", lambda e, o=out_ap, i=in_ap, f=func, k=kw: e.activation(o, i, f, **k),
                    reads=reads, writes=writes)

    def dve(fn, reads, writes):
        return P.op("dve", fn, reads=reads, writes=writes)

    def load_slab(src_ap, shape3):
        sl, sbuf_, si = nslot()
        a, b = shape3
        dst = sl[:, 0:a * b].rearrange("p (a b) -> p a b", b=b)
        dma_load("pool", dst, src_ap, [sbuf_], f"slot{si}")
        return dst, sbuf_

    def rstd_from_bank(bank, bbank, scale):
        t1, t1b = nf()
        act(t1, bank[:], AF.Ln, [bbank], [t1b], bias=epsT[:, 0:1], scale=scale)
        t2, t2b = nf()
        act(t2, t1, AF.Exp, [t1b], [t2b], scale=-0.5)
        return t2, t2b

    def norm_mod(l, which, parity, chunks=(0, 1, 2)):
        for c in chunks:
            v = 0 if c < 2 else 1
            t0 = c * CH
            bank, bb = nbank()
            for kc in range(KC):
                sq, sqb = nb()
                act(sq, xv(kc, t0, CH), AF.Square, [bx(kc, c)], [sqb])
                mm(bank[:], ones_m, sq, kc == 0, kc == KC - 1, [sqb], [bb])
            rs, rsb = rstd_from_bank(bank, bb, 1.0 / D)
            for kc in range(KC):
                A = eff[:, parity * 96 + (0 if which == 0 else 2) * 16 + kc * 2 + v: parity * 96 + (0 if which == 0 else 2) * 16 + kc * 2 + v + 1]
                mrow = (0 if which == 0 else 3) * 8 + kc
                sh = modT[:, parity * 96 + mrow * 2 + v: parity * 96 + mrow * 2 + v + 1]
                tmp, tb = nf()
                dve(lambda e, o=tmp, i0=xv(kc, t0, CH), s=A, i1=rs: e.scalar_tensor_tensor(o, i0, s, i1, ALU.mult, ALU.mult),
                    [bx(kc, c), rsb, buf("eff", parity)], [tb])
                act(hv(kc, t0, CH), tmp, AF.Identity, [tb, buf("mod", parity)], [bh(kc, c)], bias=sh)

    def post_norm_residual(l, which, parity, c, fs, fs_bufs, produce):
        v = 0 if c < 2 else 1
        t0 = c * CH
        ssb, ssbb = nacc()
        pend = None
        for dt in range(KC):
            bank, bb = produce(dt)
            if pend is not None:
                mm(ssb[:], ones_m, pend[0], pend[2] == 0, False, [pend[1]], [ssbb])
            act(fs[:, dt * CH:(dt + 1) * CH], bank[:], AF.Copy, [bb], [fs_bufs[dt]])
            sq, sqb = nb()
            act(sq, bank[:], AF.Square, [bb], [sqb])
            pend = (sq, sqb, dt)
        mm(ssb[:], ones_m, pend[0], False, True, [pend[1]], [ssbb])
        rs, rsb = rstd_from_bank(ssb, ssbb, 1.0 / D)
        for dt in range(KC):
            G = eff[:, parity * 96 + (1 if which == 0 else 3) * 16 + dt * 2 + v: parity * 96 + (1 if which == 0 else 3) * 16 + dt * 2 + v + 1]
            tmp, tb = nf()
            dve(lambda e, o=tmp, i0=fs[:, dt * CH:(dt + 1) * CH], i1=rs: e.tensor_tensor(o, i0, i1, ALU.mult),
                [fs_bufs[dt], rsb], [tb])
            dve(lambda e, o=xv(dt, t0, CH), i0=tmp, s=G: e.scalar_tensor_tensor(o, i0, s, o, ALU.mult, ALU.add),
                [tb, buf("eff", parity), bx(dt, c)], [bx(dt, c)])

    def compute_mod(l, between=None):
        parity = l % 2
        mbank, mbb = banks[7], bankb[7]
        for s in range(12):
            src = w_ada[l].rearrange("(k p) c -> p k c", p=128)[:, :, s * 512:(s + 1) * 512]
            w, wb = load_slab(src, (KC, 512))
            for ct in range(4):
                idx = s * 4 + ct
                for kc in range(KC):
                    mm(mbank[:, idx * 2: idx * 2 + 2], w[:, kc, ct * 128:(ct + 1) * 128], siluc[:, kc * 2: kc * 2 + 2],
                       kc == 0, kc == KC - 1, [wb, buf("siluc")], [mbb])
            if between is not None:
                between(s)
        mo = modT[:, parity * 96:(parity + 1) * 96]
        dve(lambda e, o=mo.rearrange("p (a b) -> p a b", b=2), i0=mbank[:, 0:96].rearrange("p (a b) -> p a b", b=2),
            i1=bada[:, l * 48:(l + 1) * 48].unsqueeze(2).broadcast_to([128, 48, 2]): e.tensor_tensor(o, i0, i1, ALU.add),
            [mbb, buf("consts")], [buf("mod", parity)])

        def gv(n):
            return gains[:, (l * 4 + n) * KC:(l * 4 + n + 1) * KC].unsqueeze(2).broadcast_to([128, KC, 2])

        def mv(m):
            return modT[:, parity * 96 + m * 16: parity * 96 + (m + 1) * 16].rearrange("p (a b) -> p a b", b=2)

        def ev(m):
            return eff[:, parity * 96 + m * 16: parity * 96 + (m + 1) * 16].rearrange("p (a b) -> p a b", b=2)

        rb, wbf = [buf("mod", parity), buf("consts")], [buf("eff", parity)]
        dve(lambda e: e.scalar_tensor_tensor(ev(0), mv(1), 1.0, gv(0), ALU.add, ALU.mult), rb, wbf)
        dve(lambda e: e.tensor_tensor(ev(1), mv(2), gv(1), ALU.mult), rb, wbf)
        dve(lambda e: e.scalar_tensor_tensor(ev(2), mv(4), 1.0, gv(2), ALU.add, ALU.mult), rb, wbf)
        dve(lambda e: e.tensor_tensor(ev(3), mv(5), gv(3), ALU.mult), rb, wbf)

    cb = buf("consts")
    for (dst, src) in ((cvec, cvec_d), (bada, bada_d), (gains, gains_d), (qkg, qkg_d), (pscale, pscale_d),
                       (cosT, cos_d), (sinT, sin_d), (biasT, bias_d), (cmats, cm_d)):
        dma_load("sp", dst[:], src[:, :], [cb], "consts")
    dve(lambda e: e.memset(epsT[:], EPS), [], [buf("eps")])
    act(siluc[:], cvec[:], AF.Silu, [cb, buf("eps")], [buf("siluc")])
    for kc in range(KC):
        dma_load("sp", xT[:, kc * T:(kc + 1) * T], x_in[kc * 128:(kc + 1) * 128, :], [bx(kc, c) for c in range(NCH)], f"xin{kc % 2}")
    compute_mod(0)

    for l in range(n_layers):
        P.epoch = 2 * l + 1
        parity = l % 2
        wl_in = w_in[l].rearrange("(k p) c -> p k c", p=128)

        wide["n"] = 8 if l > 0 else 7
        norm_mod(l, 0, parity, chunks=(0, 1) if l == 0 else (1,))

        ckb = mb("ktd_cache")
        for g in range(2):
            dma_load("pool", ktd(g, 0, 128, 0, 256), ck_d[l][:, g * 256:(g + 1) * 256], [ckb], "cachek")
        vab = [mb("vaug", t_) for t_ in range(14)]
        cvsrc = cv_d[l].rearrange("p (t g d) -> p t g d", t=2, g=2)
        for dup in range(2):
            for t_ in range(2):
                dst = U[:, O_VA + t_ * 384: O_VA + (t_ + 1) * 384].rearrange("p (g c) -> p g c", c=192)[:, :, dup * 128: dup * 128 + 64]
                dma_load("pool", dst, cvsrc[:, t_], [vab[t_]], "cachev")
        va_all = U[:, O_VA:O_VA + 14 * 384].rearrange("p (a c) -> p a c", c=192)[:, :, 64:128]
        dve(lambda e, o=va_all: e.memset(o, 1.0), [], vab)

        gq = qkg[:, l * 2: l * 2 + 1]
        gk = qkg[:, l * 2 + 1: l * 2 + 2]
        wsl = {}

        def get_w(name):
            if name not in wsl:
                if name == "q":
                    wsl[name] = load_slab(wl_in[:, :, 0:512], (KC, 512))
                else:
                    wsl[name] = load_slab(wl_in[:, :, 512:768], (KC, 256))
            return wsl[name]

        def qk_unit(is_k, j, c):
            st = {}
            t0 = c * CH
            gain = gk if is_k else gq
            dest = uv(O_KT, 0, t0, CH) if is_k else uv(O_Q, j, t0, CH)
            destb = mb("kT", c) if is_k else mb("q", j, c)

            def A():
                w, wb = get_w("kv" if is_k else "q")
                col0 = 0 if is_k else j * 128
                bank, bb = nbank()
                for kc in range(KC):
                    mm(bank[:], w[:, kc, col0:col0 + 128], hv(kc, t0, CH), kc == 0, kc == KC - 1, [wb, bh(kc, c)], [bb])
                sq, sqb = nb()
                act(sq, bank[:], AF.Square, [bb], [sqb])
                st.update(bank=bank, bb=bb, sq=sq, sqb=sqb)

            def B_():
                bank, bb = st["bank"], st["bb"]
                b2, b2b = nbank()
                mm(b2[:], bd_m, st["sq"], True, True, [st["sqb"], cb], [b2b])
                rs, rsb = rstd_from_bank(b2, b2b, 1.0 / 64)
                if is_k:
                    kf, kfb = nf()
                    dve(lambda e, o=kf, i0=bank[:], s=gain, i1=rs: e.scalar_tensor_tensor(o, i0, s, i1, ALU.mult, ALU.mult),
                        [bb, rsb, cb], [kfb])
                    dma_load("sp", k_out[l][:, t0:t0 + CH], kf, [buf("kout")], "kout", rbufs=[kfb], final=True)
                    if c == 2:
                        act(dest, kf, AF.Copy, [kfb], [destb])
                    else:
                        qn, qnb = nb()
                        act(qn, kf, AF.Copy, [kfb], [qnb])
                        st.update(qn=qn, qnb=qnb)
                else:
                    if c == 2:
                        dve(lambda e, o=dest, i0=bank[:], s=gain, i1=rs: e.scalar_tensor_tensor(o, i0, s, i1, ALU.mult, ALU.mult),
                            [bb, rsb, cb], [destb])
                    else:
                        qn, qnb = nb()
                        dve(lambda e, o=qn, i0=bank[:], s=gain, i1=rs: e.scalar_tensor_tensor(o, i0, s, i1, ALU.mult, ALU.mult),
                            [bb, rsb, cb], [qnb])
                        st.update(qn=qn, qnb=qnb)

            def C():
                if c < 2:
                    qn, qnb = st["qn"], st["qnb"]
                    b3, b3b = nbank()
                    mm(b3[:], rot_m, qn, True, True, [qnb, cb], [b3b])
                    t1, t1b = nf()
                    P.op("pool", lambda e, o=t1, i0=qn, i1=cosT[:, t0:t0 + CH]: e.tensor_tensor(o, i0, i1, ALU.mult), reads=[qnb, cb], writes=[t1b])
                    t2, t2b = nf()
                    dve(lambda e, o=t2, i0=b3[:], i1=sinT[:, t0:t0 + CH]: e.tensor_tensor(o, i0, i1, ALU.mult), [b3b, cb], [t2b])
                    dve(lambda e, o=dest, i0=t1, i1=t2: e.tensor_tensor(o, i0, i1, ALU.add), [t1b, t2b], [destb])
                if is_k:
                    k0 = 256 + t0 if c < 2 else 1280
                    kd = mb("ktd", c)
                    for g in range(2):
                        for half in range(2):
                            o = ktd(g, half * 64, half * 64 + 64, k0, CH)
                            i = uv(O_KT, 0, t0, CH, g * 64, g * 64 + 64)
                            if (g + half) % 2 == 1:
                                dve(lambda e, o=o, i=i: e.tensor_copy(o, i), [destb], [kd])
                            else:
                                act(o, i, AF.Copy, [destb], [kd])
            return [A, B_, C]

        def v_unit(tb4):
            def A():
                wkv, wkvb = get_w("kv")
                bank, bb = nbank()
                for ti in range(4):
                    i = tb4 * 4 + ti
                    for kc in range(KC):
                        mm(bank[:, ti * 128:(ti + 1) * 128], hv(kc, i * 128, 128), wkv[:, kc, 128:256], kc == 0, kc == KC - 1,
                           [wkvb, bh(kc, i // 4)], [bb])
                vs, vsb = nf()
                dve(lambda e, o=vs, i=bank[:]: e.tensor_copy(o, i), [bb], [vsb])
                dma_load("sp", v_out[l][tb4 * 512:(tb4 + 1) * 512, :].rearrange("(t p) c -> p t c", p=128),
                         vs.rearrange("p (t c) -> p t c", c=128), [buf("vout")], "vout", rbufs=[vsb], final=True)
                for dup in range(2):
                    dst = U[:, O_VA + (2 + tb4 * 4) * 384: O_VA + (2 + tb4 * 4 + 4) * 384].rearrange(
                        "p (t g c) -> p t g c", g=2, c=192)[:, :, :, dup * 128: dup * 128 + 64]
                    act(dst, vs.rearrange("p (t g d) -> p t g d", g=2, d=64), AF.Copy, [vsb], [vab[2 + tb4 * 4 + ti] for ti in range(4)])
            return [A]

        def p_unit(i, on_dve):
            def A():
                if "p" not in wsl:
                    wsl["p"] = load_slab(wl_in[:, :, 768:1280], (KC, 512))
                wp, wpb = wsl["p"]
                bank, bb = nbank()
                for kc in range(KC):
                    mm(bank[:], hv(kc, i * 128, 128), wp[:, kc, :], kc == 0, kc == KC - 1, [wpb, bh(kc, i // 4)], [bb])
                if on_dve:
                    dve(lambda e, o=ptok(i), i_=bank[:]: e.tensor_copy(o, i_), [bb], [mb("ptok", i)])
                else:
                    act(ptok(i), bank[:], AF.Copy, [bb], [mb("ptok", i)])
            return [A]

        def norm_unit(c_):
            return [lambda: norm_mod(l, 0, parity, chunks=(c_,))]

        LAG = (0, 1, 3)

        def run_units(units):
            for i in range(len(units) + LAG[2]):
                for stage in range(3):
                    u = i - LAG[stage]
                    if 0 <= u < len(units) and stage < len(units[u]):
                        units[u][stage]()

        units = [v_unit(0)] + [p_unit(i, True) for i in range(4)]
        units += [qk_unit(False, j, 0) for j in range(4)] + [qk_unit(True, 0, 0)]
        run_units(units)
        norm_mod(l, 0, parity, chunks=(2,))
        units = [qk_unit(False, j, 1) for j in range(4)] + [qk_unit(True, 0, 1)]
        units += [v_unit(1)] + [p_unit(i, i % 2 == 1) for i in range(4, 8)]
        units += [qk_unit(False, j, 2) for j in range(4)] + [qk_unit(True, 0, 2)]
        units += [v_unit(2)] + [p_unit(i, i % 2 == 1) for i in range(8, 12)]
        run_units(units)

        wpl = wplT[:, :].rearrange("p (g d) -> p g d", d=128)
        wplb = buf("wpl")
        dma_load("pool", wpl, w_pool[l].rearrange("g c d -> c g d"), [wplb], "wpl")
        mslabs = {}
        for c in range(NCH):
            for s in (2 * c, 2 * c + 1):
                sl, sbuf_, si = nslot()
                dma_load("pool", sl[:, 0:3072], mt_d[s], [sbuf_], f"slot{si}")
                mslabs[s] = (sl[:, 0:3072].rearrange("p (t g r o) -> p t g r o", t=2, g=4, r=3), sbuf_)
            for g in range(4):
                bank, bb = nbank()
                for ti in range(4):
                    i = c * 4 + ti
                    rr = [r for r in (-1, 0, 1) if not ((r == -1 and i in (0, 8, 10)) or (r == 1 and i in (7, 9, 11)))]
                    ms, msb = mslabs[i // 2]
                    for n, r in enumerate(rr):
                        mm(bank[:, ti * 128:(ti + 1) * 128], ptok(i + r)[:, g * 128:(g + 1) * 128], ms[:, i % 2, g, r + 1, :],
                           n == 0, n == len(rr) - 1, [msb, mb("ptok", i + r)], [bb])
                act(uv(O_DT, g, c * CH, CH), bank[:], AF.Copy, [bb], [mb("dT", g, c)])
                b2, b2b = nbank()
                mm(b2[:], wpl[:, g, :], uv(O_DT, g, c * CH, CH), True, True, [wplb, mb("dT", g, c)], [b2b])
                dve(lambda e, o=uv(O_YT, g, c * CH, CH), i=b2[:], s=pscale[:, l * 4 + g: l * 4 + g + 1]: e.tensor_scalar(o, i, s, None, ALU.mult),
                    [b2b, cb], [mb("yT", g, c)])

        wide["n"] = NBANK
        LOOK = 3
        tasks = []

        def mk_norm(h, c, ob, obb):
            j, hh = h // 2, h % 2
            p0, p1 = hh * 64, hh * 64 + 64
            d0, d1 = (64, 128) if hh == 0 else (0, 64)

            def fn():
                l1, l1b = nf()
                dve(lambda e, o=l1[p0:p1, :], i=ob[d0:d1, :]: e.tensor_copy(o, i), [obb], [l1b])
                act(l1[p0:p1, :], l1[p0:p1, :], AF.Ln, [l1b], [l1b])
                l2, l2b = nf()
                act(l2[p0:p1, :], l1[p0:p1, :], AF.Exp, [l1b], [l2b], scale=-1.0)
                dve(lambda e, o=uv(O_OT, j, c * CH, CH, p0, p1), i0=ob[p0:p1, :], i1=l2[p0:p1, :]: e.tensor_tensor(o, i0, i1, ALU.mult),
                    [obb, l2b], [mb("OT", j, c)])
            return fn

        for h in range(8):
            g, j, hh = h // 4, h // 2, h % 2
            p0, p1 = hh * 64, hh * 64 + 64
            vc0 = 0 if hh == 0 else 64
            for c in range(NCH):
                st = {}

                nrm = None
                if c < 2:
                    for kt in range(10):
                        def front(kt=kt, c=c, g=g, j=j, p0=p0, p1=p1, st=st):
                            if kt == 0:
                                st["ob"], st["obb"] = nacc()
                            sbk, sbkb = nbank()
                            kb_ = ckb if kt < 2 else mb("ktd", (kt - 2) // 4)
                            mm(sbk[:], ktd(g, p0, p1, kt * 128, 128), uv(O_Q, j, c * CH, CH, p0, p1), True, True,
                               [kb_, mb("q", j, c)], [sbkb])
                            pT, pTb = nb()
                            for half in range(2):
                                bi = (kt * 2 + c) * 2 + half
                                act(pT[:, half * 256:(half + 1) * 256], sbk[:, half * 256:(half + 1) * 256], AF.Exp, [sbkb, cb], [pTb],
                                    bias=biasT[:, bi:bi + 1], scale=0.125)
                            st[kt] = (pT, pTb)

                        def back(kt=kt, c=c, h=h, g=g, vc0=vc0, st=st):
                            pT, pTb = st[kt]
                            mm(st["ob"][:], vaug(kt, g, vc0, 128), pT, kt == 0, kt == 9, [vab[kt], pTb], [st["obb"]])
                            if kt == 9:
                                mk_norm(h, c, st["ob"], st["obb"])()
                        tasks.append((front, back))
                else:
                    for s_ in range(2):
                        def front(s_=s_, g=g, j=j, p0=p0, p1=p1, st=st):
                            if s_ == 0:
                                st["ob"], st["obb"] = nacc()
                            sbk, sbkb = nbank()
                            pT, pTb = nb()
                            for kt in range(2):
                                mm(sbk[:, kt * 256:(kt + 1) * 256], ktd(g, p0, p1, 1280 + s_ * 256 + kt * 128, 128),
                                   uv(O_Q, j, 1024 + s_ * 256, 256, p0, p1), True, True, [mb("ktd", 2), mb("q", j, 2)], [sbkb])
                            act(pT, sbk[:], AF.Exp, [sbkb], [pTb], scale=0.125)
                            st[s_] = (pT, pTb)

                        def back(s_=s_, h=h, g=g, vc0=vc0, st=st):
                            pT, pTb = st[s_]
                            for kt in range(2):
                                mm(st["ob"][:, s_ * 256:(s_ + 1) * 256], vaug(10 + s_ * 2 + kt, g, vc0, 128), pT[:, kt * 256:(kt + 1) * 256],
                                   kt == 0, kt == 1, [vab[10 + s_ * 2 + kt], pTb], [st["obb"]])
                            if s_ == 1:
                                mk_norm(h, 2, st["ob"], st["obb"])()
                        tasks.append((front, back))
        for i in range(len(tasks) + LOOK):
            if i < len(tasks):
                tasks[i][0]()
            if i >= LOOK:
                tasks[i - LOOK][1]()

        if debug and l == 0:
            P.op("sp", lambda e: e.dma_start(out=dbg_q[:, :], in_=U[:, O_Q:O_Q + 4 * T]), reads=[mb("q", j, c) for j in range(4) for c in range(NCH)], writes=[buf("dbgo")], dma="dbg", final=True)
            P.op("sp", lambda e: e.dma_start(out=dbg_OT[:, :], in_=U[:, O_OT:O_OT + 4 * T]), reads=[mb("OT", j, c) for j in range(4) for c in range(NCH)], writes=[buf("dbgo")], dma="dbg", final=True)
            P.op("sp", lambda e: e.dma_start(out=dbg_yT[:, :], in_=U[:, O_YT:O_YT + 4 * T]), reads=[mb("yT", j, c) for j in range(4) for c in range(NCH)], writes=[buf("dbgo")], dma="dbg", final=True)
        wide["n"] = 7
        waul = w_aup[l].rearrange("(k p) c -> p k c", p=128)
        wpul = w_pup[l].rearrange("(k p) c -> p k c", p=128)
        for dt in range(KC):
            sl, sbuf_, si = nslot()
            wgd = sl[:, 0:2048].rearrange("p (k a c) -> p k a c", a=2, c=128)
            wga = wgd[:, :, 0, :]
            wgp = wgd[:, :, 1, :]
            wau = sl[:, 2048:2560].rearrange("p (k c) -> p k c", c=128)
            wpu = sl[:, 2560:3072].rearrange("p (k c) -> p k c", c=128)
            waub = wpub = sbuf_
            dma_load("pool", wga, wl_in[:, :, 1280 + dt * 128: 1280 + (dt + 1) * 128], [sbuf_], f"slot{si}")
            dma_load("pool", wgp, wl_in[:, :, 2304 + dt * 128: 2304 + (dt + 1) * 128], [sbuf_], f"slot{si}")
            dma_load("pool", wau, waul[:, :, dt * 128:(dt + 1) * 128], [sbuf_], f"slot{si}")
            dma_load("pool", wpu, wpul[:, :, dt * 128:(dt + 1) * 128], [sbuf_], f"slot{si}")
            for c in range(NCH):
                t0 = c * CH
                ba, bab = nbank()
                for kc in range(4):
                    mm(ba[:], wau[:, kc, :], uv(O_OT, kc, t0, CH), kc == 0, kc == 3, [waub, mb("OT", kc, c)], [bab])
                bp, bpb = nbank()
                for kc in range(4):
                    mm(bp[:], wpu[:, kc, :], uv(O_YT, kc, t0, CH), kc == 0, kc == 3, [wpub, mb("yT", kc, c)], [bpb])
                bga, bgab = nbank()
                for kc in range(KC):
                    mm(bga[:], wga[:, kc, :], hv(kc, t0, CH), kc == 0, kc == KC - 1, [sbuf_, bh(kc, c)], [bgab])
                bgp, bgpb = nbank()
                for kc in range(KC):
                    mm(bgp[:], wgp[:, kc, :], hv(kc, t0, CH), kc == 0, kc == KC - 1, [sbuf_, bh(kc, c)], [bgpb])
                sa, sab = nf()
                act(sa, bga[:], AF.Sigmoid, [bgab], [sab])
                sp_, spb = nf()
                act(sp_, bgp[:], AF.Sigmoid, [bgpb], [spb])
                m1, m1b = nf()
                dve(lambda e, o=m1, i0=ba[:], i1=sa: e.tensor_tensor(o, i0, i1, ALU.mult), [bab, sab], [m1b])
                m2, m2b = nf()
                dve(lambda e, o=m2, i0=bp[:], i1=sp_: e.tensor_tensor(o, i0, i1, ALU.mult), [bpb, spb], [m2b])
                dve(lambda e, o=uv(O_MT, dt, t0, CH), i0=m1, i1=m2: e.tensor_tensor(o, i0, i1, ALU.add), [m1b, m2b], [mb("mT", dt, c)])
        fsm_bufs = [mb("fsmix", dt) for dt in range(KC)]
        if l == 0:
            for dt in range(KC):
                for c in range(NCH):
                    for j in range(4):
                        alias(mb("mT", dt, c), mb("q", j, c)) if dt < 4 and j == dt else None
                        alias(mb("mT", dt, c), mb("dT", j, c)) if dt >= 4 and j == dt - 4 else None
                for j in range(4):
                    for c in range(NCH):
                        alias(fsm_bufs[dt], mb("OT", j, c))
                        alias(fsm_bufs[dt], mb("yT", j, c))
                for i in range(12):
                    alias(fsm_bufs[dt], mb("ptok", i))
            for i in range(12):
                for j in range(4):
                    for c in range(NCH):
                        alias(mb("ptok", i), mb("OT", j, c))
            reg_alias_all()

        if debug and l == 0:
            P.op("sp", lambda e: e.dma_start(out=dbg_mT[:, :], in_=U[:, O_MT:O_MT + 8 * T]), reads=[mb("mT", j, c) for j in range(8) for c in range(NCH)], writes=[buf("dbgo")], dma="dbg", final=True)
        wide["n"] = NBANK
        wo0, wo0b = load_slab(w_out[l].rearrange("(k p) c -> p k c", p=128)[:, :, 0:512], (KC, 512))
        wo1, wo1b = load_slab(w_out[l].rearrange("(k p) c -> p k c", p=128)[:, :, 512:1024], (KC, 512))
        for c in range(NCH):
            def prod(dt, c=c):
                bank, bb = nbank()
                w, wb = (wo0, wo0b) if dt < 4 else (wo1, wo1b)
                for kc in range(KC):
                    mm(bank[:], w[:, kc, (dt % 4) * 128:(dt % 4 + 1) * 128], uv(O_MT, kc, c * CH, CH), kc == 0, kc == KC - 1,
                       [wb, mb("mT", kc, c)], [bb])
                return bank, bb
            post_norm_residual(l, 0, parity, c, fs_mix, fsm_bufs, prod)
            if c >= 1:
                norm_mod(l, 1, parity, chunks=(c - 1,))

        if debug and l == 0:
            P.op("sp", lambda e: e.dma_start(out=dbg_x1[:, :], in_=xT[:, :]), reads=[bx(kc, c) for kc in range(KC) for c in range(NCH)], writes=[buf("dbgo")], dma="dbg", final=True)
        def mod_tables_next(ln_unused):
                parity_n = (l + 1) % 2
                mo = modT[:, parity_n * 96:(parity_n + 1) * 96]
                ln = l + 1
                dve(lambda e, o=mo.rearrange("p (a b) -> p a b", b=2), i0=banks[7][:, 0:96].rearrange("p (a b) -> p a b", b=2),
                    i1=bada[:, ln * 48:(ln + 1) * 48].unsqueeze(2).broadcast_to([128, 48, 2]): e.tensor_tensor(o, i0, i1, ALU.add),
                    [bankb[7], cb], [buf("mod", parity_n)])

                def gv(n, ln=ln):
                    return gains[:, (ln * 4 + n) * KC:(ln * 4 + n + 1) * KC].unsqueeze(2).broadcast_to([128, KC, 2])

                def mv(m, pn=parity_n):
                    return modT[:, pn * 96 + m * 16: pn * 96 + (m + 1) * 16].rearrange("p (a b) -> p a b", b=2)

                def ev(m, pn=parity_n):
                    return eff[:, pn * 96 + m * 16: pn * 96 + (m + 1) * 16].rearrange("p (a b) -> p a b", b=2)

                rb, wbf = [buf("mod", parity_n), cb], [buf("eff", parity_n)]
                dve(lambda e, ev=ev, mv=mv, gv=gv: e.scalar_tensor_tensor(ev(0), mv(1), 1.0, gv(0), ALU.add, ALU.mult), rb, wbf)
                dve(lambda e, ev=ev, mv=mv, gv=gv: e.tensor_tensor(ev(1), mv(2), gv(1), ALU.mult), rb, wbf)
                dve(lambda e, ev=ev, mv=mv, gv=gv: e.scalar_tensor_tensor(ev(2), mv(4), 1.0, gv(2), ALU.add, ALU.mult), rb, wbf)
                dve(lambda e, ev=ev, mv=mv, gv=gv: e.tensor_tensor(ev(3), mv(5), gv(3), ALU.mult), rb, wbf)

        P.epoch = 2 * l + 2
        norm_mod(l, 1, parity, chunks=(2,))
        nxt_mod = (l + 1 < n_layers)
        ada_state = {"s": 0}

        def ada_slab(lnext):
            s = ada_state["s"]
            if s >= 12:
                return
            ada_state["s"] += 1
            src = w_ada[lnext].rearrange("(k p) c -> p k c", p=128)[:, :, s * 512:(s + 1) * 512]
            w, wb = load_slab(src, (KC, 512))
            for ct in range(4):
                idx = s * 4 + ct
                for kc in range(KC):
                    mm(banks[7][:, idx * 2: idx * 2 + 2], w[:, kc, ct * 128:(ct + 1) * 128], siluc[:, kc * 2: kc * 2 + 2],
                       kc == 0, kc == KC - 1, [wb, buf("siluc")], [bankb[7]])

        wgl = w_g[l].rearrange("(k p) c -> p k c", p=128)
        wul = w_u[l].rearrange("(k p) c -> p k c", p=128)
        aTb = [[buf("aT", j, c) for c in range(NCH)] for j in range(NJ)]
        if l == 0:
            for j in range(NJ):
                for c in range(NCH):
                    alias(aTb[j][c], aT_all)
        wide["n"] = 7
        for jb in range(NJ // 2):
            sl, sbuf_, si = nslot()
            wgu = sl[:, 0:4096].rearrange("p (k a c) -> p k a c", a=2, c=256)
            dma_load("pool", wgu[:, :, 0, :], wgl[:, :, jb * 256:(jb + 1) * 256], [sbuf_], f"slot{si}")
            dma_load("pool", wgu[:, :, 1, :], wul[:, :, jb * 256:(jb + 1) * 256], [sbuf_], f"slot{si}")
            for jj in range(2):
                j = jb * 2 + jj
                for c in range(NCH):
                    t0 = c * CH
                    bg, bgb = nbank()
                    for kc in range(KC):
                        mm(bg[:], wgu[:, kc, 0, jj * 128:(jj + 1) * 128], hv(kc, t0, CH), kc == 0, kc == KC - 1, [sbuf_, bh(kc, c)], [bgb])
                    bu, bub = nbank()
                    for kc in range(KC):
                        mm(bu[:], wgu[:, kc, 1, jj * 128:(jj + 1) * 128], hv(kc, t0, CH), kc == 0, kc == KC - 1, [sbuf_, bh(kc, c)], [bub])
                    sg, sgb = nf()
                    act(sg, bg[:], AF.Silu, [bgb], [sgb])
                    dve(lambda e, o=aTv(j, t0, CH), i0=sg, i1=bu[:]: e.tensor_tensor(o, i0, i1, ALU.mult), [sgb, bub], [aTb[j][c], aT_all])
            if nxt_mod:
                ada_slab(l + 1)
                if jb == 5:
                    ada_slab(l + 1)
        wide["n"] = NBANK
        if nxt_mod:
            while ada_state["s"] < 12:
                ada_slab(l + 1)
            mod_tables_next(l + 1)
        fsf_bufs = [buf("fsffn", dt) for dt in range(KC)]
        if l == 0:
            for dt in range(KC):
                for kc in range(KC):
                    for c in (1, 2):
                        alias(fsf_bufs[dt], bh(kc, c))
        for c in range(NCH):
            def prodf(dt, c=c):
                w, wb = load_slab(w_dn[l][dt].rearrange("p (j c) -> p j c", c=128), (NJ, 128))
                bank, bb = nbank()
                for j in range(NJ):
                    mm(bank[:], w[:, j, :], aTv(j, c * CH, CH), j == 0, j == NJ - 1, [wb, aTb[j][c], aT_all], [bb])
                return bank, bb
            post_norm_residual(l, 1, parity, c, fs_ffn, fsf_bufs, prodf)
            if nxt_mod and c == 1:
                norm_mod(l + 1, 0, (l + 1) % 2, chunks=(0,))
    for kc in range(KC):
        P.op("sp", lambda e, kc=kc: e.dma_start(out=y_out[kc * 128:(kc + 1) * 128, :], in_=xT[:, kc * T:(kc + 1) * T]),
             reads=[bx(kc, c) for c in range(NCH)], writes=[buf("yout")], dma="yout", final=True)

    block = stack.enter_context(nc.Block())
    run, _, _ = P.emit(nc, block, stack)

    @block.tensor
    def _(e):
        run("pe", e)

    @block.scalar
    def _(e):
        run("act", e)

    @block.vector
    def _(e):
        run("dve", e)

    @block.gpsimd
    def _(e):
        run("pool", e)

    @block.sync
    def _(e):
        run("sp", e)

    stack.close()
    return nc


POOL_WINDOWS = (2, 4, 8, 16)


def _pool_band(n_seq_tiles_list):
    ntiles = sum(n_seq_tiles_list)
    M = np.zeros((ntiles, 4, 3, 128, 128), np.float32)
    tile0 = 0
    for nt in n_seq_tiles_list:
        n = nt * 128
        for g, w in enumerate(POOL_WINDOWS):
            left = w // 2
            right = w - 1 - left
            for t in range(n):
                lo = max(t - left, 0)
                hi = min(t + right + 1, n)
                val = 1.0 / (hi - lo)
                ti, to = divmod(t, 128)
                for tp in range(lo, hi):
                    tpi, tpo = divmod(tp, 128)
                    M[tile0 + ti, g, tpi - ti + 1, tpo, to] += val
                M[tile0 + ti, g, 1, to, to] -= 1.0
        tile0 += nt
    return M


def _prep(inputs):
    f32 = np.float32
    x_prompt = np.asarray(inputs["x_prompt"], f32)
    x_sample = np.asarray(inputs["x_sample"], f32)
    cache_k = np.asarray(inputs["cache_k"], f32)
    cache_v = np.asarray(inputs["cache_v"], f32)
    c = np.asarray(inputs["c"], f32)
    c_ctx = np.asarray(inputs["c_ctx"], f32)

    shared = {}
    for k in ("w_ada", "w_in", "w_attn_up", "w_pool", "w_pool_up", "w_out", "w_ffn_gate", "w_ffn_up"):
        a = np.asarray(inputs[k], f32)
        for l in range(DEPTH):
            shared[f"{k}{l}"] = np.ascontiguousarray(a[l])
    wd = np.asarray(inputs["w_ffn_down"], f32)
    wdr = wd.reshape(DEPTH, NJ, 128, 8, 128).transpose(0, 3, 2, 1, 4).reshape(DEPTH, 8, 128, NJ * 128)
    for l in range(DEPTH):
        shared[f"w_down_r{l}"] = np.ascontiguousarray(wdr[l])
    b_ada = np.asarray(inputs["b_ada"], f32)
    shared["bada"] = np.ascontiguousarray(b_ada.reshape(DEPTH, 48, 128).transpose(2, 0, 1).reshape(128, DEPTH * 48))
    gs = np.stack([np.asarray(inputs[k], f32) for k in ("n_pre_mix", "n_post_mix", "n_pre_ffn", "n_post_ffn")], 1)
    shared["gains"] = np.ascontiguousarray(gs.reshape(DEPTH, 4, KC, 128).transpose(3, 0, 1, 2).reshape(128, DEPTH * 4 * KC))
    qn = np.asarray(inputs["q_norm"], f32)
    kn = np.asarray(inputs["k_norm"], f32)
    qk = np.stack([qn, kn], 1)
    shared["qkg"] = np.ascontiguousarray(np.tile(qk.transpose(2, 0, 1), (2, 1, 1)).reshape(128, DEPTH * 2))
    ps = np.asarray(inputs["pool_scale"], f32)
    shared["pscale"] = np.ascontiguousarray(ps.reshape(DEPTH, 4, 128).transpose(2, 0, 1).reshape(128, DEPTH * 4))
    ones = np.ones((128, 128), f32)
    bd = np.zeros((128, 128), f32)
    bd[:64, :64] = 1
    bd[64:, 64:] = 1
    rot = np.zeros((128, 128), f32)
    for hb in (0, 64):
        for d in range(32):
            rot[hb + d + 32, hb + d] = -1.0
            rot[hb + d, hb + d + 32] = 1.0
    shared["cmats"] = np.concatenate([ones, bd, rot], 1).astype(ml_dtypes.bfloat16)

    n = TS
    rows = n // 64
    row = np.repeat(np.arange(rows, dtype=f32), 64)
    col = np.tile(np.arange(64, dtype=f32), rows)
    inv = (f32(10000.0) ** (-np.arange(16, dtype=f32) / f32(16))).astype(f32)
    ang = np.concatenate([row[:, None] * inv[None, :], col[:, None] * inv[None, :]], -1).astype(f32)
    cos_s = np.cos(ang).astype(f32)
    sin_s = np.sin(ang).astype(f32)
    idx = np.arange(128) % 32
    cos_sample = np.ascontiguousarray(cos_s[:, idx].T)
    sin_sample = np.ascontiguousarray(sin_s[:, idx].T)
    cos_prompt = np.ones((128, TS), f32)
    sin_prompt = np.zeros((128, TS), f32)

    bias_sample = np.zeros((128, 40), f32)
    bias_prompt = np.full((128, 40), NEG, f32)
    for kt in range(2, 10):
        for cc in range(2):
            for half in range(2):
                if (kt - 2) // 2 == cc * 2 + half:
                    bias_prompt[:, (kt * 2 + cc) * 2 + half] = 0.0

    M_sample = np.concatenate([_pool_band([8]), _pool_band([2, 2])], 0)
    M_prompt = _pool_band([2] * 6)

    def mt_layout(M):
        return np.ascontiguousarray(M.reshape(6, 2, 4, 3, 128, 128).transpose(0, 4, 1, 2, 3, 5).reshape(6, 128, 3072)).astype(ml_dtypes.bfloat16)

    mt_sample = mt_layout(M_sample)
    mt_prompt = mt_layout(M_prompt)

    in_maps = []
    assign = []
    for core in range(8):
        m = dict(shared)
        if core < 4:
            pS = None
            pP = [2 * core, 2 * core + 1]
            xs = np.concatenate([x_sample[core]] + [x_prompt[b] for b in pP], 0)
            cS = c[core]
            m["cosT"], m["sinT"], m["biasT"], m["mtab"] = cos_sample, sin_sample, bias_sample, mt_sample
            ck = cache_k[core]
            ckt = ck.transpose(0, 2, 3, 1)
            ckd = np.concatenate([ckt, ckt], 2)
            m["cacheK"] = np.ascontiguousarray(ckd.transpose(0, 2, 1, 3).reshape(DEPTH, 128, 512))
            cv = cache_v[core]
            m["cacheV"] = np.ascontiguousarray(cv.reshape(DEPTH, 2, 128, 2, 64).transpose(0, 2, 1, 3, 4).reshape(DEPTH, 128, 256))
        else:
            base = 8 + 6 * (core - 4)
            pS = [base + i for i in range(4)]
            pP = [base + 4, base + 5]
            xs = np.concatenate([x_prompt[b] for b in pS + pP], 0)
            cS = c_ctx
            m["cosT"], m["sinT"], m["biasT"], m["mtab"] = cos_prompt, sin_prompt, bias_prompt, mt_prompt
            m["cacheK"] = np.zeros((DEPTH, 128, 512), f32)
            m["cacheV"] = np.zeros((DEPTH, 128, 256), f32)
        m["xT"] = np.ascontiguousarray(xs.T)
        cv2 = np.stack([cS, c_ctx], 1)
        m["cvec"] = np.ascontiguousarray(cv2.reshape(KC, 128, 2).transpose(1, 0, 2).reshape(128, KC * 2))
        in_maps.append(m)
        assign.append((pS, pP))
    return in_maps, assign


_NC_CACHE = {}


def kernel(**inputs):
    in_maps, assign = _prep(inputs)
    if "nc" not in _NC_CACHE:
        _NC_CACHE["nc"] = build_nc()
    nc = _NC_CACHE["nc"]
    res = run_bass_kernel_spmd(nc, in_maps, core_ids=list(range(8)))
    y_prompt = np.zeros((32, 256, D), np.float32)
    y_sample = np.zeros((4, 1024, D), np.float32)
    new_k = np.zeros((32, DEPTH, 256, 2, 64), np.float32)
    new_v = np.zeros((32, DEPTH, 256, 2, 64), np.float32)
    for core in range(8):
        r = res.results[core]
        y = np.asarray(r["yT"]).T
        kt = np.asarray(r["kT_out"])
        vt = np.asarray(r["v_out"])
        pS, pP = assign[core]
        segs = []
        if pS is None:
            y_sample[core] = y[:TS]
        else:
            segs += [(b, i * 256) for i, b in enumerate(pS)]
        segs += [(b, TS + i * 256) for i, b in enumerate(pP)]
        for b, t0 in segs:
            y_prompt[b] = y[t0:t0 + 256]
            new_k[b] = kt[:, :, t0:t0 + 256].transpose(0, 2, 1).reshape(DEPTH, 256, 2, 64)
            new_v[b] = vt[:, t0:t0 + 256, :].reshape(DEPTH, 256, 2, 64)
    return (y_prompt, y_sample, new_k, new_v)
```
